# Optimizing a Trainium2 kernel written in Bass

```python
import math
import jax, jax.numpy as jnp
from jax import lax
import numpy as np

D_MODEL = 1024
BATCH = 16
SEQ = 2048
DEPTH = 1
DEC_BATCH = 4
DEC_SEQ = 8192
PAST_LEN = 128

GRID_W = 64
N_HEADS = 8
N_KV_HEADS = 2
HEAD_DIM = 64
Q_PER_KV = N_HEADS // N_KV_HEADS
ATTN_WIDTH = N_HEADS * HEAD_DIM
KV_WIDTH = N_KV_HEADS * HEAD_DIM
AXIS_DIM = HEAD_DIM // 2
ROPE_THETA = 10000.0
Q_BLOCK = 128
HG_HEADS = 4
HG_EXPAND = 128
HG_WIDTH = HG_HEADS * HG_EXPAND
HG_CHUNK = 64
N_MEM = 256
X_HEADS = 4
X_HEAD_DIM = D_MODEL // X_HEADS
D_FF = 4 * D_MODEL
EPS = 1e-6
ALPHA = (2 * DEPTH) ** 0.25
BETA = (8 * DEPTH) ** -0.25
IN_SIZES = (ATTN_WIDTH, KV_WIDTH, KV_WIDTH, HG_WIDTH, HG_WIDTH, HG_WIDTH, HG_WIDTH, HG_WIDTH, D_MODEL, D_MODEL)
IN_WIDTH = ATTN_WIDTH + 2 * KV_WIDTH + 5 * HG_WIDTH + 2 * D_MODEL

kernel_name = 'hybrid_gqa_hgrn2_encoder'


def layer_norm(x, g, b):
    xf = x.astype(jnp.float32)
    mu = jnp.mean(xf, axis=-1, keepdims=True)
    var = jnp.mean(jnp.square(xf - mu), axis=-1, keepdims=True)
    return ((xf - mu) * lax.rsqrt(var + EPS) * g.astype(jnp.float32) + b.astype(jnp.float32)).astype(x.dtype)


def rms_norm(x, g):
    xf = x.astype(jnp.float32)
    return (xf * lax.rsqrt(jnp.mean(xf * xf, axis=-1, keepdims=True) + EPS) * g.astype(jnp.float32)).astype(x.dtype)


def split_columns(h, sizes):
    parts, start = [], 0
    for s in sizes:
        parts.append(h[..., start:start + s])
        start += s
    return parts


def axial_rope_angles(T):
    rows = T // GRID_W
    row = jnp.repeat(jnp.arange(rows, dtype=jnp.float32), GRID_W)
    col = jnp.tile(jnp.arange(GRID_W, dtype=jnp.float32), rows)
    inv_freq = ROPE_THETA ** (-jnp.arange(0, AXIS_DIM, 2, dtype=jnp.float32) / AXIS_DIM)
    return row[:, None] * inv_freq, col[:, None] * inv_freq


def rotate_half(x, ang):
    m = x.shape[-1] // 2
    cos = jnp.cos(ang)[:, None, :]
    sin = jnp.sin(ang)[:, None, :]
    x1 = x[..., :m].astype(jnp.float32)
    x2 = x[..., m:].astype(jnp.float32)
    return jnp.concatenate([x1 * cos - x2 * sin, x1 * sin + x2 * cos], axis=-1)


def apply_axial_rope(x, ang_row, ang_col):
    out = jnp.concatenate([rotate_half(x[..., :AXIS_DIM], ang_row), rotate_half(x[..., AXIS_DIM:], ang_col)], axis=-1)
    return out.astype(x.dtype)


def blocked_gqa_attention(q, k, v):
    B, T = q.shape[:2]
    nb = T // Q_BLOCK
    qb = q.reshape(B, nb, Q_BLOCK, N_KV_HEADS, Q_PER_KV, HEAD_DIM).transpose(1, 0, 3, 4, 2, 5)
    kh = k.transpose(0, 2, 1, 3)
    vh = v.transpose(0, 2, 1, 3)
    scale = 1.0 / math.sqrt(HEAD_DIM)

    def attend(qblk):
        s = jnp.einsum('bkgqd,bksd->bkgqs', qblk, kh).astype(jnp.float32) * scale
        p = jax.nn.softmax(s, axis=-1).astype(vh.dtype)
        return jnp.einsum('bkgqs,bksd->bkgqd', p, vh)

    o = lax.map(attend, qb)
    return o.transpose(1, 0, 4, 2, 3, 5).reshape(B, T, ATTN_WIDTH)


def hgrn2_chunk_scan(q, k, v, log_f):
    B, T, H, dk = q.shape
    dv = v.shape[-1]
    nc = T // HG_CHUNK

    def to_chunks(a):
        return a.reshape(B, nc, HG_CHUNK, H, a.shape[-1]).transpose(1, 0, 3, 2, 4)

    causal = jnp.tril(jnp.ones((HG_CHUNK, HG_CHUNK), dtype=bool))[:, :, None]

    def step(S, inp):
        qc, kc, vc, gc = inp
        b = jnp.cumsum(gc, axis=2)
        o_inter = jnp.einsum('bhtd,bhde->bhte', qc * jnp.exp(b), S)
        rel = jnp.where(causal, b[:, :, :, None, :] - b[:, :, None, :, :], -jnp.inf)
        scores = jnp.einsum('bhtd,bhsd,bhtsd->bhts', qc, kc, jnp.exp(rel))
        o_intra = jnp.einsum('bhts,bhse->bhte', scores, vc)
        b_end = b[:, :, -1:, :]
        S = jnp.exp(b_end[:, :, 0, :, None]) * S + jnp.einsum('bhsd,bhse->bhde', kc * jnp.exp(b_end - b), vc)
        return S, o_inter + o_intra

    S0 = jnp.zeros((B, H, dk, dv), jnp.float32)
    _, o = lax.scan(step, S0, (to_chunks(q), to_chunks(k), to_chunks(v), to_chunks(log_f)))
    return o.transpose(1, 0, 3, 2, 4).reshape(B, T, H, dv)


def token_mixer(x, w_in, w_pa, w_pb, w_out, q_norm, k_norm, lb, g_norm):
    B, T, _ = x.shape
    q_a, k_a, v_a, q_h, zf_fw, zf_bw, i_h, g_h, gate_a, gate_b = split_columns(x @ w_in, IN_SIZES)
    ang_row, ang_col = axial_rope_angles(T)
    q_a = apply_axial_rope(rms_norm(q_a.reshape(B, T, N_HEADS, HEAD_DIM), q_norm), ang_row, ang_col)
    k_a = apply_axial_rope(rms_norm(k_a.reshape(B, T, N_KV_HEADS, HEAD_DIM), k_norm), ang_row, ang_col)
    v_a = v_a.reshape(B, T, N_KV_HEADS, HEAD_DIM)
    o_a = blocked_gqa_attention(q_a, k_a, v_a)
    def heads(a):
        return a.reshape(B, T, HG_HEADS, HG_EXPAND).astype(jnp.float32)
    q_h = jax.nn.silu(heads(q_h))
    i_h = heads(i_h)
    lb = lb.reshape(2, HG_HEADS, HG_EXPAND)
    f_fw = lb[0] + (1.0 - lb[0]) * jax.nn.sigmoid(heads(zf_fw))
    f_bw = lb[1] + (1.0 - lb[1]) * jax.nn.sigmoid(heads(zf_bw))
    o_fw = hgrn2_chunk_scan(q_h, 1.0 - f_fw, i_h, jnp.log(f_fw))
    rev = lambda a: jnp.flip(a, axis=1)
    o_bw = rev(hgrn2_chunk_scan(rev(q_h), rev(1.0 - f_bw), rev(i_h), rev(jnp.log(f_bw))))
    o_h = rms_norm(o_fw + o_bw, g_norm) * jax.nn.silu(heads(g_h))
    o_b = o_h.reshape(B, T, HG_WIDTH).astype(x.dtype)
    merged = jax.nn.sigmoid(gate_a) * (o_a @ w_pa) + jax.nn.sigmoid(gate_b) * (o_b @ w_pb)
    return merged @ w_out


def memory_cross_attention(x, mem, w_q, w_k, w_v, w_o):
    B, T, _ = x.shape
    q = (x @ w_q).reshape(B, T, X_HEADS, X_HEAD_DIM)
    k = (mem @ w_k).reshape(B, N_MEM, X_HEADS, X_HEAD_DIM)
    v = (mem @ w_v).reshape(B, N_MEM, X_HEADS, X_HEAD_DIM)
    s = jnp.einsum('bthd,bmhd->bhtm', q, k).astype(jnp.float32) * (1.0 / math.sqrt(X_HEAD_DIM))
    p = jax.nn.softmax(s, axis=-1).astype(v.dtype)
    o = jnp.einsum('bhtm,bmhd->bthd', p, v).reshape(B, T, D_MODEL)
    return o @ w_o


def squared_relu_mlp(x, w_up, w_down):
    return jnp.square(jax.nn.relu(x @ w_up)) @ w_down


def setup_inputs(seed: int = 0) -> dict:
    key = jax.random.key(seed)
    ks = jax.random.split(key, 24)
    nrm = lambda k, shape: jax.random.normal(k, shape, jnp.float32)
    def w(k, shape, fan_in, scale=1.0):
        return nrm(k, shape) * (scale * fan_in ** -0.5)
    def gain(k, shape):
        return 1.0 + 0.02 * nrm(k, shape)
    def bias(k, shape):
        return 0.02 * nrm(k, shape)
    return {
        'x_prompt': nrm(ks[0], (BATCH, SEQ, D_MODEL)),
        'x_sample': nrm(ks[1], (DEC_BATCH, DEC_SEQ, D_MODEL)),
        'mem_prompt': nrm(ks[2], (BATCH, N_MEM, D_MODEL)),
        'mem_sample': nrm(ks[3], (DEC_BATCH, N_MEM, D_MODEL)),
        'w_in': w(ks[4], (DEPTH, D_MODEL, IN_WIDTH), D_MODEL),
        'w_pa': w(ks[5], (DEPTH, ATTN_WIDTH, D_MODEL), ATTN_WIDTH),
        'w_pb': w(ks[6], (DEPTH, HG_WIDTH, D_MODEL), HG_WIDTH),
        'w_out': w(ks[7], (DEPTH, D_MODEL, D_MODEL), D_MODEL, BETA),
        'q_norm': gain(ks[8], (DEPTH, HEAD_DIM)),
        'k_norm': gain(ks[9], (DEPTH, HEAD_DIM)),
        'hg_lb': 0.1 * nrm(ks[10], (2, DEPTH + 1, HG_WIDTH)),
        'hg_gnorm': gain(ks[11], (DEPTH, HG_EXPAND)),
        'ln1_g': gain(ks[12], (DEPTH, D_MODEL)),
        'ln1_b': bias(ks[13], (DEPTH, D_MODEL)),
        'w_xq': w(ks[14], (DEPTH, D_MODEL, D_MODEL), D_MODEL),
        'w_xk': w(ks[15], (DEPTH, D_MODEL, D_MODEL), D_MODEL),
        'w_xv': w(ks[16], (DEPTH, D_MODEL, D_MODEL), D_MODEL, BETA),
        'w_xo': w(ks[17], (DEPTH, D_MODEL, D_MODEL), D_MODEL, BETA),
        'ln2_g': gain(ks[18], (DEPTH, D_MODEL)),
        'ln2_b': bias(ks[19], (DEPTH, D_MODEL)),
        'w_up': w(ks[20], (DEPTH, D_MODEL, D_FF), D_MODEL),
        'w_down': w(ks[21], (DEPTH, D_FF, D_MODEL), D_FF, BETA),
        'ln3_g': gain(ks[22], (DEPTH, D_MODEL)),
        'ln3_b': bias(ks[23], (DEPTH, D_MODEL)),
    }


def reference(x_prompt, x_sample, mem_prompt, mem_sample, w_in, w_pa, w_pb, w_out, q_norm, k_norm, hg_lb, hg_gnorm, ln1_g, ln1_b, w_xq, w_xk, w_xv, w_xo, ln2_g, ln2_b, w_up, w_down, ln3_g, ln3_b):
    lb_all = jnp.cumsum(jax.nn.softmax(hg_lb.astype(jnp.float32), axis=1), axis=1)

    def run(x, mem):
        for l in range(DEPTH):
            x = layer_norm(ALPHA * x + token_mixer(x, w_in[l], w_pa[l], w_pb[l], w_out[l], q_norm[l], k_norm[l], lb_all[:, l], hg_gnorm[l]), ln1_g[l], ln1_b[l])
            x = layer_norm(ALPHA * x + memory_cross_attention(x, mem, w_xq[l], w_xk[l], w_xv[l], w_xo[l]), ln2_g[l], ln2_b[l])
            x = layer_norm(ALPHA * x + squared_relu_mlp(x, w_up[l], w_down[l]), ln3_g[l], ln3_b[l])
        return x

    y_prompt = run(x_prompt, mem_prompt)
    y_sample = run(x_sample, mem_sample)
    return (y_prompt, y_sample)
```

```python
import contextlib
import numpy as np
import concourse.bass as bass
import concourse.mybir as mybir
from concourse.bass_utils import run_bass_kernel_spmd

F32 = mybir.dt.float32
BF16 = mybir.dt.bfloat16
AF = mybir.ActivationFunctionType
ALU = mybir.AluOpType
AX = mybir.AxisListType

D = 1024
NCORES = 8
EPS = 1e-6
ALPHA = 2.0 ** 0.25
IN_W = 5376
C_QA, C_KA, C_VA, C_QH, C_ZF, C_ZB, C_IH, C_GH, C_GA, C_GB = 0, 512, 640, 768, 1280, 1792, 2304, 2816, 3328, 4352
HEAD_PERM = (0, 4, 1, 5, 2, 6, 3, 7)

ENGS = ("pe", "act", "dve", "pool", "sp")
SEM_EPOCH = 30000


class Buf:
    __slots__ = ("name", "w", "r", "rd", "excl")

    def __init__(self, name, excl=False):
        self.name = name
        self.excl = excl
        self.w = None
        self.r = {}
        self.rd = []


class Op:
    __slots__ = ("eng", "fn", "deps", "marked", "cnt", "dma", "dkey", "dcnt")

    def __init__(self, eng, fn, dma=False, dkey=None):
        self.eng = eng
        self.fn = fn
        self.deps = []
        self.marked = False
        self.cnt = 0
        self.dma = dma
        self.dkey = dkey
        self.dcnt = 0


class Sched:
    def __init__(self):
        self.ops = {e: [] for e in ENGS}
        self.dma_counts = {}
        self.total_keys = set()

    def op(self, eng, fn, reads=(), writes=(), dma=False, dkey=None):
        o = Op(eng, fn, dma, dkey)
        deps = []
        for b in reads:
            if b.w is not None:
                deps.append(b.w)
            if b.excl:
                deps.extend(v for k, v in b.r.items() if k != eng)
        for b in writes:
            if b.w is not None:
                deps.append(b.w)
            deps.extend(b.r.values())
            deps.extend(b.rd)
        for b in writes:
            b.w = o
            b.r = {}
            b.rd = []
        for b in reads:
            if b.w is not o:
                if dma:
                    b.rd.append(o)
                else:
                    b.r[eng] = o
        seen = set()
        for d in deps:
            if d is o or id(d) in seen:
                continue
            seen.add(id(d))
            if d.eng == "pe" and eng == "pe" and not d.dma and not dma:
                continue
            if dma and d.dma and d.dkey == dkey and dkey in self.total_keys:
                continue
            o.deps.append(d)
        if dma:
            self.dma_counts[dkey] = self.dma_counts.get(dkey, 0) + 1
            o.dcnt = self.dma_counts[dkey]
        self.ops[eng].append(o)
        return o

    def emit(self, nc):
        for e in ENGS:
            for o in self.ops[e]:
                for d in o.deps:
                    if not d.dma:
                        d.marked = True
        nsem = {}
        for e in ENGS:
            c = 0
            for o in self.ops[e]:
                if o.marked and not o.dma:
                    c += 1
                    o.cnt = c
            nsem[e] = max(1, (c + SEM_EPOCH - 1) // SEM_EPOCH)
        with contextlib.ExitStack() as st:
            esem = {e: [st.enter_context(nc.semaphore("s_%s_%d" % (e, i))) for i in range(nsem[e])] for e in ENGS}
            dsem = {k: st.enter_context(nc.semaphore("d_%s" % str(k))) for k in self.dma_counts}
            block = st.enter_context(nc.Block())
            engobj = {"pe": "tensor", "act": "scalar", "dve": "vector", "pool": "gpsimd", "sp": "sync"}

            def body(e):
                def f(eng):
                    waited = {}
                    for o in self.ops[e]:
                        for d in o.deps:
                            if d.dma:
                                key = ("d", d.dkey)
                                val = 16 * (self.dma_counts[d.dkey] if d.dkey in self.total_keys else d.dcnt)
                                sem = dsem[d.dkey]
                            else:
                                ep = (d.cnt - 1) // SEM_EPOCH
                                key = ("e", d.eng, ep)
                                val = d.cnt - ep * SEM_EPOCH
                                sem = esem[d.eng][ep]
                            if waited.get(key, 0) >= val:
                                continue
                            waited[key] = val
                            eng.wait_ge(sem, val)
                        ins = o.fn(eng)
                        if o.dma:
                            ins.then_inc(dsem[o.dkey], 16)
                        elif o.marked:
                            ins.then_inc(esem[e][(o.cnt - 1) // SEM_EPOCH], 1)
                    if e == "sp":
                        for k, c in self.dma_counts.items():
                            eng.wait_ge(dsem[k], 16 * c)
                return f

            for e in ENGS:
                getattr(block, engobj[e])(body(e))


class KB:
    def __init__(self, nc, st):
        self.nc = nc
        self.st = st
        self.S = Sched()
        self.ring_i = 0
        self.ps_i = 0
        self.acc_i = 0

    def sb(self, name, shape, dt):
        return self.st.enter_context(self.nc.sbuf_tensor(name, shape, dt))

    def mm(self, out, lhsT, rhs, start, stop, R, W):
        self.S.op("pe", lambda e: e.matmul(out, lhsT=lhsT, rhs=rhs, start=start, stop=stop), R, W)

    def tr(self, out, in_, ident, R, W):
        self.S.op("pe", lambda e: e.transpose(out=out, in_=in_, identity=ident), R, W)

    def act(self, out, in_, func, R, W, scale=None, bias=None):
        kw = {}
        if scale is not None:
            kw["scale"] = scale
        if bias is not None:
            kw["bias"] = bias
        self.S.op("act", lambda e: e.activation(out=out, in_=in_, func=func, **kw), R, W)

    def tt(self, eng, out, in0, in1, op, R, W):
        self.S.op(eng, lambda e: e.tensor_tensor(out=out, in0=in0, in1=in1, op=op), R, W)

    def ts(self, eng, out, in0, s1, s2, op0, op1, R, W):
        if op1 is None:
            self.S.op(eng, lambda e: e.tensor_scalar(out=out, in0=in0, scalar1=s1, scalar2=None, op0=op0), R, W)
        else:
            self.S.op(eng, lambda e: e.tensor_scalar(out=out, in0=in0, scalar1=s1, scalar2=s2, op0=op0, op1=op1), R, W)

    def stt(self, out, in0, scalar, in1, op0, op1, R, W):
        self.S.op("dve", lambda e: e.scalar_tensor_tensor(out=out, in0=in0, scalar=scalar, in1=in1, op0=op0, op1=op1), R, W)

    def cp(self, eng, out, in_, R, W):
        if eng == "act":
            self.S.op("act", lambda e: e.activation(out=out, in_=in_, func=AF.Copy), R, W)
        else:
            self.S.op(eng, lambda e: e.tensor_copy(out=out, in_=in_), R, W)

    def dma(self, out, in_, R, W, key, eng="sp", **kw):
        self.S.op(eng, lambda e: e.dma_start(out=out, in_=in_, **kw), R, W, dma=True, dkey=key)


def build(TP, TS, NPJ=2):
    TO = TS // 2
    nc = bass.Bass("TRN2", target_bir_lowering=False, dynamic_dma_scratch_size=4096)

    def din(name, shape, dt=F32):
        return nc.dram_tensor(name, list(shape), dt, kind="ExternalInput").ap()

    xp = din("xp", [NPJ, TP, D]); xs = din("xs", [TS, D])
    memp = din("memp", [NPJ, 256, D]); mems = din("mems", [256, D])
    rope_p = din("rope_p", [TP, 128]); rope_s = din("rope_s", [TS, 128])
    wd = {}
    wshape = {"w_in": [D, IN_W], "w_pa": [512, D], "w_pb": [512, D], "w_out": [D, D], "w_xq": [D, D],
              "w_xk": [D, D], "w_xv": [D, D], "w_xo": [D, D], "w_up": [D, 4 * D], "w_down": [4 * D, D]}
    wbf = {}
    for n, s in wshape.items():
        wd[n] = din(n, s)
        wbf[n] = nc.dram_tensor(n + "_bf", list(s), BF16, kind="Internal").ap()
    gq = din("gq", [1, 64]); gk = din("gk", [1, 64]); gn = din("gn", [1, 128])
    lbin = din("lbin", [4, 512])
    lnp = din("lnp", [6, D])
    cst = din("cst", [128, 5 * 128 + 4])
    yp = nc.dram_tensor("yp", [NPJ, TP, D], F32, kind="ExternalOutput").ap()
    ys = nc.dram_tensor("ys", [TO, D], F32, kind="ExternalOutput").ap()

    NKT = TS // 128
    NOT_ = max(TO, TP) // 128

    with contextlib.ExitStack() as st:
        kb = KB(nc, st)
        S = kb.S
        sb = kb.sb
        KT = sb("KT", [128, NKT * 128], BF16); bKT = Buf("KT")
        VV = sb("VV", [128, NKT, 192], BF16); bVV = Buf("VV")
        OBW = sb("OBW", [128, NOT_, 512], BF16); bOBW = [Buf("OBW%d" % i) for i in range(NOT_)]
        KmT = sb("KmT", [128, 8, 256], BF16); bKmT = Buf("KmT")
        Vm = sb("Vm", [128, 2, D], BF16); bVm = Buf("Vm")
        LNP = sb("LNP", [128, 6, D], F32); bC = Buf("consts")
        LB = sb("LB", [128, 4, 512], F32)
        GQ = sb("GQ", [128, 64], F32); GK = sb("GK", [128, 64], F32); GN = sb("GN", [128, 128], F32)
        CST = sb("CST", [128, 5 * 128 + 4], F32)
        IDB = sb("IDB", [128, 128], BF16); ONES = sb("ONES", [128, 128], BF16)
        IDF = CST[:, 0:128]
        Wc = {0: CST[:, 128:256], 1: CST[:, 256:384]}
        Mk = {0: CST[:, 384:512], 1: CST[:, 512:640]}
        Wme = {0: CST[:, 640:642], 1: CST[:, 642:644]}
        NR = 3
        RING = [sb("RING%d" % i, [128, 4096], BF16) for i in range(NR)]
        bRING = [Buf("ring%d" % i) for i in range(NR)]
        XRES = sb("XRES", [128, 4, D], F32); bXRES = [Buf("xres%d" % i) for i in range(4)]
        XT = sb("XT", [128, 8, 512], BF16); bXT = Buf("XT")
        RR = sb("RR", [128, 12288], BF16)
        bRR = Buf("RR")
        QT = RR[:, 0:2048].rearrange("p (j t) -> p j t", t=512); bQT = Buf("QT")
        OAT = RR[:, 2048:4096].rearrange("p (j t) -> p j t", t=512); bOAT = Buf("OAT")
        OXT = RR[:, 0:4096].rearrange("p (j t) -> p j t", t=512)
        OBT = RR[:, 4096:6144].rearrange("p (j t) -> p j t", t=512); bOBT = Buf("OBT")
        MG = RR[:, 6144:10240].rearrange("p (j t) -> p j t", t=512); bMG = Buf("MG")
        PT = [RR[:, 10240 + 512 * i:10240 + 512 * (i + 1)] for i in range(4)]; bPT = [Buf("PT%d" % i) for i in range(4)]
        HT = RR[:, 0:8192].rearrange("p (j t) -> p j t", t=512)
        QZ = MG
        SCR = sb("SCR", [128, 8, 512], F32); bSCR = [Buf("scr%d" % i) for i in range(8)]
        LBT = SCR[:, 0:4, :]
        SCB = sb("SCB", [128, 8, 512], BF16); bSCB = [Buf("scb%d" % i) for i in range(8)]
        SST = sb("SST", [128, 2, 512], F32); bSST = [Buf("sst0"), Buf("sst1")]
        ROPE = sb("ROPE", [128, 4, 2, 64], F32); bROPE = [Buf("rope%d" % i) for i in range(4)]
        SM = sb("SM", [128, 96], F32); bSM = Buf("SM")
        bSMb = Buf("SMb"); bSMd = Buf("SMd"); bSMc = Buf("SMc"); bSMe = Buf("SMe")
        HSF = [RING[i][:, :].bitcast(F32)[:, 512 * q:512 * (q + 1)] for i in range(NR) for q in range(4)]
        bHS = [Buf("hs%d" % i) for i in range(12)]
        HSB9 = RING[2][:, 2048:2560]
        DUM = sb("DUM", [128, 2], F32)
        XIN = XRES; bXIN = [bXRES[0], bXRES[1]]
        XTTs = [XRES[:, 2 + i, :].bitcast(BF16)[:, 0:1024].rearrange("p (k t) -> p k t", t=128) for i in range(2)]
        bXTT = [bXRES[2], bXRES[3]]
        XTF = XT[:].rearrange("p k t -> p (k t)")
        HWq = RR[:, 0:4096].rearrange("p (k c) -> p k c", c=512); bHWq = [bQT, bOAT]
        HWf = RR[:, 4096:8192].rearrange("p (k c) -> p k c", c=512); bHWf = [bOBT, bMG]
        HWi = RR[:, 8192:12288].rearrange("p (k c) -> p k c", c=512); bHWi = [bMG] + bPT
        HWk = XTF[:, 0:2048].rearrange("p (k c) -> p k c", c=256)
        HWg = XTF[:, 0:4096].rearrange("p (k c) -> p k c", c=512)
        PSB = [st.enter_context(nc.psum_tensor("PS%d" % i, [128, 512], F32)) for i in range(8)]
        bPS = [Buf("ps%d" % i, excl=True) for i in range(8)]
        bW = Buf("wscratch")

        state = {"ps": 0, "acc": 0, "ring": 0, "xin": 0, "rope": 0, "pt": 0, "main": False}

        def psg():
            if state["main"]:
                i = state["ps"] % 4
            else:
                i = state["ps"] % 8
            state["ps"] += 1
            return PSB[i], bPS[i]

        def psacc():
            i = 4 + state["acc"] % 4
            state["acc"] += 1
            return PSB[i], bPS[i]

        def wload(dram3d, shape):
            i = state["ring"] % NR
            state["ring"] += 1
            a, b = shape
            dst = RING[i][:, 0:a * b].rearrange("p (a b) -> p a b", b=b)
            kb.dma(dst, dram3d, [bW], [bRING[i]], ("ring", i))
            return dst, bRING[i]

        S.total_keys.add("wcast")
        S.total_keys.add("const")
        for n, s in wshape.items():
            tot = s[0] * s[1]
            rows = tot // 512
            src = wd[n].rearrange("a (b f) -> (a b) f", f=512) if s[1] % 512 == 0 else None
            if src is None:
                src = wd[n].rearrange("a (b f) -> (a b) f", f=256)
                dst = wbf[n].rearrange("a (b f) -> (a b) f", f=256)
                rows = tot // 256
            else:
                dst = wbf[n].rearrange("a (b f) -> (a b) f", f=512)
            r0 = 0
            while r0 < rows:
                r1 = min(rows, r0 + 2048)
                kb.dma(dst[r0:r1, :], src[r0:r1, :], [], [bW], "wcast", eng="pool")
                r0 = r1
        kb.dma(CST[:], cst[:, :], [], [bC], "const")
        for i in range(6):
            kb.dma(LNP[:, i, :], lnp[i:i + 1, :].broadcast_to([128, D]), [], [bC], "const")
        for i in range(4):
            kb.dma(LBT[:, i, :], lbin[i:i + 1, :].broadcast_to([128, 512]), [], [bSCR[i]], "const")
        kb.dma(GQ[:], gq[0:1, :].broadcast_to([128, 64]), [], [bC], "const")
        kb.dma(GK[:], gk[0:1, :].broadcast_to([128, 64]), [], [bC], "const")
        kb.dma(GN[:], gn[0:1, :].broadcast_to([128, 128]), [], [bC], "const")
        bC2 = Buf("consts2")
        for d in range(2):
            kb.tt("dve", LBT[:, 2 * d, :], LBT[:, 2 * d, :], LBT[:, 2 * d + 1, :], ALU.subtract, [bSCR[2 * d], bSCR[2 * d + 1]], [bSCR[2 * d]])
            kb.act(LB[:, 2 * d, :], LBT[:, 2 * d, :], AF.Sigmoid, [bSCR[2 * d]], [bC2])
            kb.act(LB[:, 2 * d + 1, :], LBT[:, 2 * d, :], AF.Sigmoid, [bSCR[2 * d]], [bC2], scale=-1.0)
        kb.cp("dve", IDB[:], IDF, [bC], [bC2])
        S.op("pool", lambda e: e.memset(ONES[:], 1.0), [], [bC2])
        S.op("pool", lambda e: e.memset(VV[:, :, 64:128], 1.0), [], [bVV])

        def load_x_transpose(xtile_ap, xbuf, R_extra, dst_fn, dst_buf):
            for g in range(2):
                ps, bps = psg()
                for kk in range(4):
                    k = 4 * g + kk
                    kb.tr(ps[:, kk * 128:(kk + 1) * 128], xtile_ap[:, k * 128:(k + 1) * 128], IDF, [xbuf, bC] + R_extra, [bps])
                kb.cp("act" if g == 0 else "dve", dst_fn(g), ps[:].rearrange("p (k t) -> p k t", t=128), [bps], [dst_buf])

        def proj_tok(ps_ap, bps, xt_fn, xbuf, w3, wbuf, ncols):
            for k in range(8):
                kb.mm(ps_ap[:, 0:ncols], xt_fn(k), w3[:, k, 0:ncols], k == 0, k == 7, [xbuf] + (wbuf if isinstance(wbuf, list) else [wbuf]), [bps])

        def gen_norm_rope(zps, bz, nh, G, rope_i, out_ap, out_buf, scr, so=0, bsm=None):
            bsm = bsm or bSM
            n = nh * 64
            (A0, b0), (A1, b1), (A2, b2) = scr
            T0 = A0[:, 0:n]; T1 = A1[:, 0:n]; T2 = A2[:, 0:n]
            kb.act(T0, zps[:, 0:n], AF.Square, [bz], [b0])
            ss = SM[:, so:so + nh]; rs = SM[:, 16 + so:16 + so + nh]
            yield
            kb.tt("dve", T1.rearrange("p (h d) -> p h d", d=64), zps[:, 0:n].rearrange("p (h d) -> p h d", d=64),
                  G[:, 0:64].unsqueeze(1).broadcast_to([128, nh, 64]), ALU.mult, [bz, bC], [b1])
            S.op("dve", lambda e: e.tensor_reduce(out=ss, in_=T0.rearrange("p (h d) -> p h d", d=64), axis=AX.X, op=ALU.add), [b0], [bsm])
            kb.ts("dve", ss, ss, 1.0 / 64.0, EPS, ALU.mult, ALU.add, [bsm], [bsm])
            kb.act(ss, ss, AF.Ln, [bsm], [bsm])
            kb.act(rs, ss, AF.Exp, [bsm], [bsm], scale=-0.5)
            yield
            Cb = ROPE[:, rope_i, 0, :].unsqueeze(1).broadcast_to([128, nh, 64])
            kb.tt("pool", T0.rearrange("p (h d) -> p h d", d=64), T1.rearrange("p (h d) -> p h d", d=64), Cb, ALU.mult,
                  [b1, bROPE[rope_i]], [b0])
            x5 = T1.rearrange("p (h a b d) -> p h a b d", a=2, b=2, d=16)
            r5 = T2.rearrange("p (h a b d) -> p h a b d", a=2, b=2, d=16)
            sg = ROPE[:, rope_i, 1, :].rearrange("p (a b d) -> p a b d", a=2, b=2)
            for b_ in range(2):
                sgb = sg[:, :, b_, :].unsqueeze(1).broadcast_to([128, nh, 2, 16])
                kb.tt("dve", r5[:, :, :, b_, :], x5[:, :, :, 1 - b_, :], sgb, ALU.mult, [b1, bROPE[rope_i]], [b2])
            yield
            kb.tt("pool", T0, T0, T2, ALU.add, [b0, b2], [b0])
            yield
            kb.tt("pool", out_ap.rearrange("p (h d) -> p h d", d=64), T0.rearrange("p (h d) -> p h d", d=64),
                  rs.unsqueeze(2).broadcast_to([128, nh, 64]), ALU.mult, [b0, bsm], [out_buf])

        def norm_rope(*a, **k):
            for _ in gen_norm_rope(*a, **k):
                pass

        def layer_norm(tile, ln_i):
            xr = XRES[:, tile, :]
            st6 = SM[:, 32:44]; mv = SM[:, 44:46]; rstd = SM[:, 46:47]
            for h in range(2):
                S.op("dve", lambda e, h=h: e.bn_stats(out=SM[:, 32 + 6 * h:38 + 6 * h], in_=xr[:, h * 512:(h + 1) * 512]), [bXRES[tile]], [bSMc])
            S.op("dve", lambda e: e.bn_aggr(out=mv, in_=st6), [bSMc], [bSMc])
            kb.ts("dve", rstd, SM[:, 45:46], EPS, None, ALU.add, None, [bSMc], [bSMc])
            kb.act(rstd, rstd, AF.Ln, [bSMc], [bSMc])
            kb.act(rstd, rstd, AF.Exp, [bSMc], [bSMc], scale=-0.5)
            kb.stt(xr, xr, SM[:, 44:45], LNP[:, 2 * ln_i, :], ALU.subtract, ALU.mult, [bXRES[tile], bSMc, bC], [bXRES[tile]])
            kb.stt(xr, xr, rstd, LNP[:, 2 * ln_i + 1, :], ALU.mult, ALU.add, [bXRES[tile], bSMc, bC], [bXRES[tile]])

        def roundrobin(*gens):
            gl = [g for g in gens if g is not None]
            while gl:
                for g in list(gl):
                    try:
                        next(g)
                    except StopIteration:
                        gl.remove(g)

        def F(i):
            return (SCR[:, i, :], bSCR[i]) if i < 8 else (HSF[i - 8], bHS[i - 8])
        T_F = F(0)
        T_LF = [F(1), F(8)]
        T_K = [F(2), F(9)]
        T_Q = [F(3), F(10)]
        T_EP = F(4)
        T_EM = F(5)
        T_R0 = F(6)
        T_R1 = F(7)
        NRS = [F(11), F(12), F(13)]
        T_TU = F(14)
        T_TS = F(17)
        T_G = [F(15), F(16)]
        B_V = [(SCB[:, 0, :], bSCB[0]), (HSB9, bHS[9])]
        B_QE, B_KE, B_QET, B_KET, B_ATM, B_SP = [(SCB[:, i, :], bSCB[i]) for i in range(1, 7)]
        CE = SM[:, 48:56]
        CF = SM[:, 56:60]

        def gen_A(sweep, tile, own, par, x_ap, rope_ap, W, nxt):
            wkv, bwkv, wqh, bwqh, wzf, bwzf, wih, bwih, wgh, bwgh = W
            xi = par
            load_x_transpose(XIN[:, xi, :], bXIN[xi], [], lambda g: XTTs[xi][:, 4 * g:4 * g + 4, :], bXTT[xi])
            xt_fn = lambda k: XTTs[xi][:, k, :]
            if nxt is not None:
                kb.dma(XIN[:, 1 - xi, :], x_ap[nxt * 128:(nxt + 1) * 128, :], [], [bXIN[1 - xi]], ("xin", 1 - xi))
            yield
            if sweep == 1:
                pkv, bpkv = psg()
                proj_tok(pkv, bpkv, xt_fn, bXTT[xi], wkv, bwkv, 256)
                ri = state["rope"] % 4; state["rope"] += 1
                kb.dma(ROPE[:, ri, :, :].rearrange("p c d -> p (c d)"), rope_ap[tile * 128:(tile + 1) * 128, :],
                       [], [bROPE[ri]], ("rope", ri))
                yield
                KB_ = SCB[:, 7, 0:128]
                norm_rope(pkv, bpkv, 2, GK, ri, KB_, bSCB[7], NRS)
                kb.cp("act", VV[:, tile, 0:64], pkv[:, 128:192], [bpkv], [bVV])
                kb.cp("act", VV[:, tile, 128:192], pkv[:, 192:256], [bpkv], [bVV])
                yield
            if own:
                zq, bzq = psg()
                proj_tok(zq, bzq, xt_fn, bXTT[xi], wqh, bwqh, 512)
                yield
            zf, bzf = psg()
            proj_tok(zf, bzf, xt_fn, bXTT[xi], wzf, bwzf, 512)
            if sweep == 0:
                zg, bzg = psg()
                proj_tok(zg, bzg, xt_fn, bXTT[xi], wgh, bwgh, 512)
            if own:
                kb.act(T_Q[par][0], zq[:], AF.Silu, [bzq], [T_Q[par][1]])
            if sweep == 0:
                kb.act(T_G[par][0], zg[:], AF.Silu, [bzg], [T_G[par][1]])
            kb.act(T_F[0], zf[:], AF.Sigmoid, [bzf], [T_F[1]])
            yield
            if sweep == 1:
                pt_, bpt_ = psg()
                ptb = pt_[:].bitcast(BF16)
                kb.tr(ptb[:, 0:128], SCB[:, 7, 0:128], IDB[:], [bSCB[7], bC2], [bpt_])
                kb.cp("dve", KT[:, tile * 128:(tile + 1) * 128], ptb[:, 0:128], [bpt_], [bKT])
            zi, bzi = psg()
            proj_tok(zi, bzi, xt_fn, bXTT[xi], wih, bwih, 512)
            kb.tt("dve", T_F[0], T_F[0], LB[:, 2 * sweep + 1, :], ALU.mult, [T_F[1], bC2], [T_F[1]])
            kb.tt("pool", T_F[0], T_F[0], LB[:, 2 * sweep, :], ALU.add, [T_F[1], bC2], [T_F[1]])
            kb.act(T_LF[par][0], T_F[0], AF.Ln, [T_F[1]], [T_LF[par][1]])
            kb.ts("pool", T_K[par][0], T_F[0], -1.0, 1.0, ALU.mult, ALU.add, [T_F[1]], [T_K[par][1]])
            kb.cp("act", B_V[par][0], zi[:], [bzi], [B_V[par][1]])
            if sweep == 0:
                tg = T_G[par]
                kb.tt("dve", tg[0].rearrange("p (h d) -> p h d", d=128), tg[0].rearrange("p (h d) -> p h d", d=128),
                      GN[:].unsqueeze(1).broadcast_to([128, 4, 128]), ALU.mult, [tg[1], bC], [tg[1]])
            yield

        def gen_B(dr, tile, own, par, first):
            lf, blf = T_LF[par]
            tk, btk = T_K[par]
            tq, btq = T_Q[par]
            bv, bbv = B_V[par]
            pb, bpb = psg()
            kb.mm(pb[:], Wc[dr], lf, True, True, [bC, blf], [bpb])
            pc, bpc = psg()
            for h in range(4):
                kb.mm(pc[:, 2 * h:2 * h + 2], lf[:, h * 128:(h + 1) * 128], Wme[dr], True, True, [bC, blf], [bpc])
            yield
            kb.act(CE, pc[:, 0:8], AF.Exp, [bpc], [bSMb])
            ce3 = CE.rearrange("p (h c) -> p h c", c=2)
            kb.tt("dve", CF, ce3[:, :, 0], ce3[:, :, 1], ALU.mult, [bSMb], [bSMb])
            kb.act(T_EM[0], pb[:], AF.Exp, [bpb], [T_EM[1]], scale=-1.0)
            kb.tt("pool", B_KE[0], tk, T_EM[0], ALU.mult, [btk, T_EM[1]], [B_KE[1]])
            Ops = bO = None
            if own:
                kb.act(T_EP[0], pb[:], AF.Exp, [bpb], [T_EP[1]])
                kb.tt("pool", B_QE[0], tq, T_EP[0], ALU.mult, [btq, T_EP[1]], [B_QE[1]])
                yield
                for (src, dst, eng_) in ((B_QE, B_QET, "act"), (B_KE, B_KET, "dve")):
                    ps, bps = psg()
                    psb = ps[:].bitcast(BF16)
                    for h in range(4):
                        kb.tr(psb[:, h * 128:(h + 1) * 128], src[0][:, h * 128:(h + 1) * 128], IDB[:], [src[1], bC2], [bps])
                    kb.cp(eng_, dst[0], psb[:, 0:512], [bps], [dst[1]])
                yield
                pa, bpa = psg()
                for h in range(4):
                    hs = slice(h * 128, (h + 1) * 128)
                    kb.mm(pa[:, hs], B_KET[0][:, hs], B_QET[0][:, hs], True, True, [B_KET[1], B_QET[1]], [bpa])
                kb.tt("dve", B_ATM[0].rearrange("p (h t) -> p h t", t=128), pa[:].rearrange("p (h t) -> p h t", t=128),
                      Mk[dr].unsqueeze(1).broadcast_to([128, 4, 128]), ALU.mult, [bpa, bC], [B_ATM[1]])
                if not first:
                    kb.tt("pool", B_SP[0].rearrange("p (h e) -> p h e", e=128), SST[:, dr, :].rearrange("p (h e) -> p h e", e=128),
                          ce3[:, :, 0:1].broadcast_to([128, 4, 128]), ALU.mult, [bSST[dr], bSMb], [B_SP[1]])
                yield
                Ops, bO = psg()
                for h in range(4):
                    hs = slice(h * 128, (h + 1) * 128)
                    kb.mm(Ops[:, hs], B_ATM[0][:, hs], bv[:, hs], True, first, [B_ATM[1], bbv], [bO])
                    if not first:
                        kb.mm(Ops[:, hs], B_QET[0][:, hs], B_SP[0][:, hs], False, True, [B_QET[1], B_SP[1]], [bO])
            yield
            pk, bpk = psg()
            for h in range(4):
                hs = slice(h * 128, (h + 1) * 128)
                kb.mm(pk[:, hs], B_KE[0][:, hs], bv[:, hs], True, True, [B_KE[1], bbv], [bpk])
            cend_b = ce3[:, :, 1:2].broadcast_to([128, 4, 128])
            pk3 = pk[:].rearrange("p (h e) -> p h e", e=128)
            S3 = SST[:, dr, :].rearrange("p (h e) -> p h e", e=128)
            if first:
                kb.tt("dve", S3, pk3, cend_b, ALU.mult, [bpk, bSMb], [bSST[dr]])
            else:
                kb.tt("dve", T_TU[0].rearrange("p (h e) -> p h e", e=128), pk3, cend_b, ALU.mult, [bpk, bSMb], [T_TU[1]])
                kb.tt("pool", T_TS[0].rearrange("p (h e) -> p h e", e=128), S3, CF.unsqueeze(2).broadcast_to([128, 4, 128]), ALU.mult,
                      [bSST[dr], bSMb], [T_TS[1]])
                kb.tt("pool", SST[:, dr, :], T_TS[0], T_TU[0], ALU.add, [T_TS[1], T_TU[1]], [bSST[dr]])
            yield
            if own and dr == 1:
                kb.cp("act", OBW[:, tile, :], Ops[:], [bO], [bOBW[tile]])
            if dr == 0:
                T0, b0 = T_R0
                T1, b1 = T_R1
                kb.tt("dve", T0, Ops[:], OBW[:, tile, :], ALU.add, [bO, bOBW[tile]], [b0])
                kb.tt("pool", T1, T0, T0, ALU.mult, [b0], [b1])
                ss = SM[:, 64:68]; rs = SM[:, 72:76]
                S.op("dve", lambda e: e.tensor_reduce(out=ss, in_=T1.rearrange("p (h d) -> p h d", d=128), axis=AX.X, op=ALU.add),
                     [b1], [bSMd])
                kb.ts("dve", ss, ss, 1.0 / 128.0, EPS, ALU.mult, ALU.add, [bSMd], [bSMd])
                kb.act(ss, ss, AF.Ln, [bSMd], [bSMd])
                kb.act(rs, ss, AF.Exp, [bSMd], [bSMd], scale=-0.5)
                kb.tt("pool", T0.rearrange("p (h d) -> p h d", d=128), T0.rearrange("p (h d) -> p h d", d=128),
                      rs.unsqueeze(2).broadcast_to([128, 4, 128]), ALU.mult, [b0, bSMd], [b0])
                kb.tt("pool", OBW[:, tile, :], T0, T_G[par][0], ALU.mult, [b0, T_G[par][1]], [bOBW[tile]])

        def run_job(ji, x_ap, T_all, T_own, mem_ap, rope_ap, out_ap):
            n_all = T_all // 128
            n_own = T_own // 128
            nblk_all = T_all // 512
            nblk_own = T_own // 512
            state["main"] = False
            for mt in range(2):
                xi = state["xin"] % 2; state["xin"] += 1
                kb.dma(XIN[:, xi, :], mem_ap[mt * 128:(mt + 1) * 128, :], [], [bXIN[xi]], ("xin", xi))
                load_x_transpose(XIN[:, xi, :], bXIN[xi], [],
                                 lambda g, mt=mt: XT[:, 4 * g:4 * g + 4, mt * 128:(mt + 1) * 128], bXT)
            for half in range(2):
                w3, wb_ = wload(wbf["w_xk"][:, half * 512:(half + 1) * 512].rearrange("(k p) c -> p k c", p=128), (8, 512))
                for mm_ in range(4):
                    m = half * 4 + mm_
                    ps, bps = psg()
                    for k in range(8):
                        kb.mm(ps[:, 0:256], w3[:, k, mm_ * 128:(mm_ + 1) * 128], XT[:, k, 0:256], k == 0, k == 7, [wb_, bXT], [bps])
                    kb.cp("act", KmT[:, m, :], ps[:, 0:256], [bps], [bKmT])
            for half in range(2):
                w3, wb_ = wload(wbf["w_xv"][:, half * 512:(half + 1) * 512].rearrange("(k p) c -> p k c", p=128), (8, 512))
                for mt in range(2):
                    ps, bps = psg()
                    for k in range(8):
                        kb.mm(ps[:], XT[:, k, mt * 128:(mt + 1) * 128], w3[:, k, :], k == 0, k == 7, [wb_, bXT], [bps])
                    kb.cp("dve", Vm[:, mt, half * 512:(half + 1) * 512], ps[:], [bps], [bVm])

            S.op("dve", lambda e: e.memset(DUM[:, 0:1], 0.0), [], bRING + bHS)
            for sweep in (1, 0):
                blocks = list(range(nblk_all)) if sweep == 1 else list(range(nblk_own))
                if sweep == 1:
                    blocks = blocks[::-1]
                first = True
                win = wbf["w_in"]
                wv = lambda c0, n: win[:, c0:c0 + n].rearrange("(k p) c -> p k c", p=128)
                if sweep == 1:
                    kb.dma(HWq, wv(C_QH, 512), [bW], bHWq, ("hw", 0))
                    kb.dma(HWf, wv(C_ZB, 512), [bW], bHWf, ("hw", 1))
                    kb.dma(HWi, wv(C_IH, 512), [bW], bHWi, ("hw", 2))
                    kb.dma(HWk, wv(C_KA, 256), [bW], [bXT], ("hw", 3))
                else:
                    kb.dma(HWf, wv(C_ZF, 512), [bW], bHWf, ("hw", 1))
                    kb.dma(HWg, wv(C_GH, 512), [bW], [bXT], ("hw", 3))
                wkv, bwkv = HWk, [bXT]
                wqh, bwqh = HWq, bHWq
                wzf, bwzf = HWf, bHWf
                wih, bwih = HWi, bHWi
                wgh, bwgh = HWg, [bXT]
                order = []
                for blk in blocks:
                    tl = list(range(4 * blk, 4 * blk + 4))
                    order.extend(tl[::-1] if sweep == 1 else tl)
                prevB = None
                kb.dma(XIN[:, 0, :], x_ap[order[0] * 128:(order[0] + 1) * 128, :], [], [bXIN[0]], ("xin", 0))
                for idx, tile in enumerate(order):
                    par = idx % 2
                    nxt = order[idx + 1] if idx + 1 < len(order) else None
                    gA = gen_A(sweep, tile, tile < n_own, par, x_ap, rope_ap, (wkv, bwkv, wqh, bwqh, wzf, bwzf, wih, bwih, wgh, bwgh), nxt)
                    roundrobin(gA, prevB)
                    prevB = gen_B(sweep, tile, tile < n_own, par, idx == 0)
                roundrobin(prevB)
            S.op("dve", lambda e: e.memset(DUM[:, 1:2], 0.0), [], bRING + bHS)

            state["main"] = True
            for blk in range(nblk_own):
                t0 = blk * 512
                if blk == 0:
                    for t in range(4):
                        kb.dma(XRES[:, t, :], x_ap[t0 + t * 128:t0 + (t + 1) * 128, :], [], [bXRES[t]], ("xres", t))
                for t in range(4):
                    load_x_transpose(XRES[:, t, :], bXRES[t], [], lambda g, t=t: XT[:, 4 * g:4 * g + 4, t * 128:(t + 1) * 128], bXT)
                qz4_ = QZ.rearrange("p (j h) t -> p j h t", h=2)
                S.op("pool", lambda e, a=qz4_[64:128, :, 0, :]: e.memset(a, 0.0), [], [bMG])
                S.op("pool", lambda e, a=qz4_[0:64, :, 1, :]: e.memset(a, 0.0), [], [bMG])
                wq, bwq = wload(wbf["w_in"][:, C_QA:C_QA + 512].rearrange("(k p) c -> p k c", p=128), (8, 512))
                pqs = []
                for t in range(4):
                    pq, bpq = psg()
                    proj_tok(pq, bpq, lambda k, t=t: XT[:, k, t * 128:(t + 1) * 128], bXT, wq, bwq, 512)
                    ri = state["rope"] % 4; state["rope"] += 1
                    kb.dma(ROPE[:, ri, :, :].rearrange("p c d -> p (c d)"), rope_ap[t0 + t * 128:t0 + (t + 1) * 128, :],
                           [], [bROPE[ri]], ("rope", ri))
                    pqs.append((pq, bpq, ri))
                for t in range(4):
                    tile = blk * 4 + t
                    ps, bps = psacc()
                    psb = ps[:].bitcast(BF16)
                    for h in range(4):
                        kb.tr(psb[:, h * 128:(h + 1) * 128], OBW[:, tile, h * 128:(h + 1) * 128], IDB[:], [bOBW[tile], bC2], [bps])
                    kb.cp("act", OBT[:, :, t * 128:(t + 1) * 128], psb[:, 0:512].rearrange("p (j t) -> p j t", t=128), [bps], [bOBT])
                nrscr = [[(SCR[:, 6, :], bSCR[6]), (SCR[:, 7, :], bSCR[7]), (SCR[:, 5, :], bSCR[5])],
                         [(SCR[:, 2, :], bSCR[2]), (SCR[:, 3, :], bSCR[3]), (SCR[:, 0, :], bSCR[0])]]
                for pr in range(2):
                    gl = []
                    for q_ in range(2):
                        t = 2 * pr + q_
                        pq, bpq, ri = pqs[t]
                        gl.append(gen_norm_rope(pq, bpq, 8, GQ, ri, SCB[:, 7 - q_, :], bSCB[7 - q_], nrscr[q_], so=8 * q_,
                                                bsm=(bSM if q_ == 0 else bSMe)))
                    roundrobin(*gl)
                    for q_ in range(2):
                        t = 2 * pr + q_
                        ps, bps = psacc()
                        psb = ps[:].bitcast(BF16)
                        for j in range(4):
                            kb.tr(psb[:, j * 128:(j + 1) * 128], SCB[:, 7 - q_, j * 128:(j + 1) * 128], IDB[:], [bSCB[7 - q_], bC2], [bps])
                        tc_ = slice(t * 128, (t + 1) * 128)
                        ps3 = psb[:, 0:512].rearrange("p (j t) -> p j t", t=128)
                        qz4 = QZ.rearrange("p (j h) t -> p j h t", h=2)
                        kb.cp("act", qz4[0:64, :, 0, tc_], ps3[0:64, :, :], [bps], [bMG])
                        kb.cp("act", qz4[64:128, :, 1, tc_], ps3[64:128, :, :], [bps], [bMG])
                RCP = SCR[:, 4, :]
                for j in range(4):
                    for half in range(2):
                        prt = slice(half * 64, half * 64 + 64)
                        oprt = slice(64 - half * 64, 128 - half * 64)
                        oa, boa = psacc()
                        LOOK = 2
                        pend = []
                        for kt in range(n_all + LOOK):
                            if kt < n_all:
                                sp_, bsp = psg()
                                kb.mm(sp_[:], KT[:, kt * 128:(kt + 1) * 128], QZ[:, 2 * j + half, :], True, True, [bKT, bMG], [bsp])
                                pi = state["pt"] % 4; state["pt"] += 1
                                kb.act(PT[pi], sp_[:], AF.Exp, [bsp], [bPT[pi]], scale=0.125)
                                pend.append(pi)
                            if kt >= LOOK:
                                k2 = kt - LOOK
                                pi2 = pend[k2]
                                kb.mm(oa[:], VV[:, k2, half * 64:half * 64 + 128], PT[pi2], k2 == 0, k2 == n_all - 1, [bVV, bPT[pi2]], [boa])
                        S.op("dve", lambda e, oa=oa, oprt=oprt: e.reciprocal(out=RCP[oprt, :], in_=oa[oprt, :]), [boa], [bSCR[4]])
                        kb.tt("dve", OAT[prt, j, :], oa[prt, :], RCP[oprt, :], ALU.mult, [boa, bSCR[4]], [bOAT])
                for br in range(2):
                    wp_, bwp_ = wload(wbf["w_pa" if br == 0 else "w_pb"].rearrange("(k p) c -> p k c", p=128), (4, 1024))
                    src3, bsrc = (OAT, bOAT) if br == 0 else (OBT, bOBT)
                    cg = C_GA if br == 0 else C_GB
                    for mh in range(2):
                        wg_, bwg_ = wload(wbf["w_in"][:, cg + mh * 512:cg + (mh + 1) * 512].rearrange("(k p) c -> p k c", p=128), (8, 512))
                        for mm_ in range(4):
                            m = mh * 4 + mm_
                            pg_, bpg_ = psg()
                            for k in range(8):
                                kb.mm(pg_[:], wg_[:, k, mm_ * 128:(mm_ + 1) * 128], XT[:, k, :], k == 0, k == 7, [bwg_, bXT], [bpg_])
                            pp_, bpp_ = psg()
                            for k in range(4):
                                kb.mm(pp_[:], wp_[:, k, m * 128:(m + 1) * 128], src3[:, k, :], k == 0, k == 3, [bwp_, bsrc], [bpp_])
                            kb.act(SCR[:, 0, :], pg_[:], AF.Sigmoid, [bpg_], [bSCR[0]])
                            if br == 0:
                                kb.tt("dve", MG[:, m, :], pp_[:], SCR[:, 0, :], ALU.mult, [bpp_, bSCR[0]], [bMG])
                            else:
                                kb.tt("dve", SCR[:, 1, :], pp_[:], SCR[:, 0, :], ALU.mult, [bpp_, bSCR[0]], [bSCR[1]])
                                kb.tt("dve", MG[:, m, :], MG[:, m, :], SCR[:, 1, :], ALU.add, [bMG, bSCR[1]], [bMG])

                def xT_of(t):
                    load_x_transpose(XRES[:, t, :], bXRES[t], [], lambda g, t=t: XT[:, 4 * g:4 * g + 4, t * 128:(t + 1) * 128], bXT)

                def out_proj_ln(src3, bsrc, wname, ln_i, nk, after):
                    ws = [wload(wbf[wname][:, half * 512:(half + 1) * 512].rearrange("(k p) c -> p k c", p=128), (nk, 512)) for half in range(2)]
                    for t in range(4):
                        for half in range(2):
                            w3, wb_ = ws[half]
                            ps, bps = psg()
                            for k in range(nk):
                                kb.mm(ps[:], src3[:, k, t * 128:(t + 1) * 128], w3[:, k, :], k == 0, k == nk - 1, bsrc + [wb_], [bps])
                            xr = XRES[:, t, half * 512:(half + 1) * 512]
                            kb.stt(xr, xr, ALPHA, ps[:], ALU.mult, ALU.add, [bXRES[t], bps], [bXRES[t]])
                        layer_norm(t, ln_i)
                        if t >= 1:
                            after(t - 1)
                    after(3)

                out_proj_ln(MG, [bMG], "w_out", 0, 8, xT_of)
                for half in range(2):
                    w3, wb_ = wload(wbf["w_xq"][:, half * 512:(half + 1) * 512].rearrange("(k p) c -> p k c", p=128), (8, 512))
                    for mm_ in range(4):
                        m = half * 4 + mm_
                        ps, bps = psg()
                        for k in range(8):
                            kb.mm(ps[:], w3[:, k, mm_ * 128:(mm_ + 1) * 128], XT[:, k, :], k == 0, k == 7, [wb_, bXT], [bps])
                        kb.cp("act", MG[:, m, :], ps[:], [bps], [bMG])
                for hx in range(4):
                    pis = []
                    for mt in range(2):
                        sp_, bsp = psg()
                        for dc in range(2):
                            kb.mm(sp_[:], KmT[:, 2 * hx + dc, mt * 128:(mt + 1) * 128], MG[:, 2 * hx + dc, :], dc == 0, dc == 1, [bKmT, bMG], [bsp])
                        pi = state["pt"] % 4; state["pt"] += 1
                        kb.act(PT[pi], sp_[:], AF.Exp, [bsp], [bPT[pi]], scale=1.0 / 16.0)
                        pis.append(pi)
                    sm_, bsm_ = psacc()
                    for mt in range(2):
                        kb.mm(sm_[:], ONES[:], PT[pis[mt]], mt == 0, mt == 1, [bC2, bPT[pis[mt]]], [bsm_])
                    S.op("dve", lambda e, sm_=sm_: e.reciprocal(out=SCR[:, 6, :], in_=sm_[:]), [bsm_], [bSCR[6]])
                    for dc in range(2):
                        ox, box = psacc()
                        for mt in range(2):
                            kb.mm(ox[:], Vm[:, mt, (2 * hx + dc) * 128:(2 * hx + dc + 1) * 128], PT[pis[mt]], mt == 0, mt == 1, [bVm, bPT[pis[mt]]], [box])
                        kb.tt("dve", OXT[:, 2 * hx + dc, :], ox[:], SCR[:, 6, :], ALU.mult, [box, bSCR[6]], [bQT, bOAT])
                out_proj_ln(OXT, [bQT, bOAT], "w_xo", 1, 8, xT_of)
                bHT = [bQT, bOAT, bOBT, bMG]
                for ffh in range(2):
                    for c4 in range(4):
                        c0 = ffh * 2048 + c4 * 512
                        w3, wb_ = wload(wbf["w_up"][:, c0:c0 + 512].rearrange("(k p) c -> p k c", p=128), (8, 512))
                        for mm_ in range(4):
                            fc = c4 * 4 + mm_
                            ps, bps = psg()
                            for k in range(8):
                                kb.mm(ps[:], w3[:, k, mm_ * 128:(mm_ + 1) * 128], XT[:, k, :], k == 0, k == 7, [wb_, bXT], [bps])
                            kb.act(SCR[:, 7, :], ps[:], AF.Square, [bps], [bSCR[7]])
                            kb.stt(HT[:, fc, :], ps[:], 0.0, SCR[:, 7, :], ALU.is_gt, ALU.mult, [bps, bSCR[7]], bHT)
                    for half in range(2):
                        accs = [psacc() for _ in range(4)]
                        for c2 in range(2):
                            r0 = ffh * 2048 + c2 * 1024
                            w3, wb_ = wload(wbf["w_down"][r0:r0 + 1024, half * 512:(half + 1) * 512].rearrange("(k p) c -> p k c", p=128), (8, 512))
                            for kk in range(8):
                                k = c2 * 8 + kk
                                for t in range(4):
                                    kb.mm(accs[t][0][:], HT[:, k, t * 128:(t + 1) * 128], w3[:, kk, :], k == 0, k == 15, bHT + [wb_], [accs[t][1]])
                        for t in range(4):
                            xr = XRES[:, t, half * 512:(half + 1) * 512]
                            if ffh == 0:
                                kb.stt(xr, xr, ALPHA, accs[t][0][:], ALU.mult, ALU.add, [bXRES[t], accs[t][1]], [bXRES[t]])
                            else:
                                kb.tt("dve", xr, xr, accs[t][0][:], ALU.add, [bXRES[t], accs[t][1]], [bXRES[t]])
                for t in range(4):
                    layer_norm(t, 2)
                    kb.dma(out_ap[t0 + t * 128:t0 + (t + 1) * 128, :], XRES[:, t, :], [bXRES[t]], [], ("xout", t))
                    if blk + 1 < nblk_own:
                        t1_ = t0 + 512
                        kb.dma(XRES[:, t, :], x_ap[t1_ + t * 128:t1_ + (t + 1) * 128, :], [], [bXRES[t]], ("xres", t))

        for j in range(NPJ):
            run_job(j, xp[j], TP, TP, memp[j], rope_p, yp[j])
        run_job(NPJ, xs, TS, TO, mems, rope_s, ys)
        S.emit(nc)
    return nc


def _rope_tables(pos, T):
    rows = (pos // 64).astype(np.float32)
    cols = (pos % 64).astype(np.float32)
    inv = (np.float32(10000.0) ** (-(np.arange(0, 32, 2, dtype=np.float32)) / np.float32(32))).astype(np.float32)
    ar = (rows[:, None] * inv[None, :]).astype(np.float32)
    ac = (cols[:, None] * inv[None, :]).astype(np.float32)
    cr, sr, cc, sc = np.cos(ar), np.sin(ar), np.cos(ac), np.sin(ac)
    C = np.concatenate([cr, cr, cc, cc], axis=1)
    Sg = np.concatenate([-sr, sr, -sc, sc], axis=1)
    return np.ascontiguousarray(np.concatenate([C, Sg], axis=1).astype(np.float32))


def _consts():
    s = np.arange(128)[:, None]
    t = np.arange(128)[None, :]
    ident = (s == t).astype(np.float32)
    Wf = (s <= t).astype(np.float32) - (s <= 63).astype(np.float32)
    Wb = (s >= t).astype(np.float32) - (s >= 64).astype(np.float32)
    Mf = (s <= t).astype(np.float32)
    Mb = (s >= t).astype(np.float32)
    sv = np.arange(128)
    wme = np.stack([(sv <= 63), (sv >= 64), (sv >= 64), (sv <= 63)], axis=1).astype(np.float32)
    return np.ascontiguousarray(np.concatenate([ident, Wf, Wb, Mf, Mb, wme], axis=1))


_CACHE = {}
_HOOK = {}


def kernel(x_prompt, x_sample, mem_prompt, mem_sample, w_in, w_pa, w_pb, w_out, q_norm, k_norm, hg_lb, hg_gnorm,
           ln1_g, ln1_b, w_xq, w_xk, w_xv, w_xo, ln2_g, ln2_b, w_up, w_down, ln3_g, ln3_b):
    f = lambda a: np.ascontiguousarray(np.asarray(a, dtype=np.float32))
    x_prompt, x_sample, mem_prompt, mem_sample = f(x_prompt), f(x_sample), f(mem_prompt), f(mem_sample)
    NB, TP, _ = x_prompt.shape
    NS, TS, _ = x_sample.shape
    NPJ = NB // NCORES
    assert NS * 2 == NCORES
    key = (TP, TS, NPJ)
    if key not in _CACHE:
        _CACHE[key] = build(TP, TS, NPJ)
    nc = _CACHE[key]
    w_in0 = f(w_in)[0]
    qcols = np.concatenate([np.arange(h * 64, (h + 1) * 64) for h in HEAD_PERM])
    parts = [w_in0[:, 0:512][:, qcols], w_in0[:, 512:768]]
    segs = {"qh": w_in0[:, 768:1280], "zf": w_in0[:, 1280:1792], "zb": w_in0[:, 1792:2304], "rest": w_in0[:, 2304:]}
    w_in_nat = np.ascontiguousarray(np.concatenate(parts + [segs["qh"], segs["zf"], segs["zb"], segs["rest"]], axis=1))
    w_in_rev = np.ascontiguousarray(np.concatenate(parts + [segs["qh"], segs["zb"], segs["zf"], segs["rest"]], axis=1))
    w_pa_p = np.ascontiguousarray(f(w_pa)[0][qcols, :])
    lb = f(hg_lb)
    lb_nat = np.ascontiguousarray(lb.reshape(4, 512))
    lb_rev = np.ascontiguousarray(lb[::-1].reshape(4, 512))
    lnp = np.ascontiguousarray(np.concatenate([f(ln1_g), f(ln1_b), f(ln2_g), f(ln2_b), f(ln3_g), f(ln3_b)], axis=0))
    common = {
        "w_pa": w_pa_p, "w_pb": f(w_pb)[0], "w_out": f(w_out)[0], "w_xq": f(w_xq)[0], "w_xk": f(w_xk)[0],
        "w_xv": f(w_xv)[0], "w_xo": f(w_xo)[0], "w_up": f(w_up)[0], "w_down": f(w_down)[0],
        "gq": np.ascontiguousarray(f(q_norm)[0][None, :]), "gk": np.ascontiguousarray(f(k_norm)[0][None, :]),
        "gn": np.ascontiguousarray(f(hg_gnorm)[0][None, :]), "lnp": lnp, "cst": _consts(),
    }
    TO = TS // 2
    in_maps = []
    for c in range(NCORES):
        rev = (c % 2 == 1)
        s = c // 2
        xp = x_prompt[c * NPJ:(c + 1) * NPJ]
        xs = x_sample[s]
        pp = np.arange(TP)
        ps_ = np.arange(TS)
        if rev:
            xp = xp[:, ::-1, :]
            xs = xs[::-1, :]
            pp = pp[::-1]
            ps_ = ps_[::-1]
        m = dict(common)
        m.update({
            "xp": np.ascontiguousarray(xp), "xs": np.ascontiguousarray(xs),
            "memp": np.ascontiguousarray(mem_prompt[c * NPJ:(c + 1) * NPJ]), "mems": np.ascontiguousarray(mem_sample[s]),
            "rope_p": _rope_tables(pp, TP), "rope_s": _rope_tables(ps_, TS),
            "w_in": w_in_rev if rev else w_in_nat, "lbin": lb_rev if rev else lb_nat,
        })
        in_maps.append(m)
    if _HOOK.get("sim") is not None:
        return _HOOK["sim"](nc, in_maps)
    res = run_bass_kernel_spmd(nc, in_maps, core_ids=list(range(NCORES)))
    y_prompt = np.empty((NB, TP, D), np.float32)
    y_sample = np.empty((NS, TS, D), np.float32)
    for c in range(NCORES):
        r = res.results[c]
        rev = (c % 2 == 1)
        yp = r["yp"]
        ys = r["ys"]
        s = c // 2
        if rev:
            y_prompt[c * NPJ:(c + 1) * NPJ] = yp[:, ::-1, :]
            y_sample[s, TO:] = ys[::-1, :]
        else:
            y_prompt[c * NPJ:(c + 1) * NPJ] = yp
            y_sample[s, :TO] = ys
    return (y_prompt, y_sample)
```

```python
import contextlib
import numpy as np
import concourse.bass as bass
import concourse.mybir as mybir
from concourse.bass_utils import run_bass_kernel_spmd

F32 = mybir.dt.float32
BF16 = mybir.dt.bfloat16
AF = mybir.ActivationFunctionType
ALU = mybir.AluOpType
AX = mybir.AxisListType

D = 1024
NCORES = 8
EPS = 1e-6
ALPHA = 2.0 ** 0.25
IN_W = 5376
C_QA, C_KA, C_VA, C_QH, C_ZF, C_ZB, C_IH, C_GH, C_GA, C_GB = 0, 512, 640, 768, 1280, 1792, 2304, 2816, 3328, 4352
HEAD_PERM = (0, 4, 1, 5, 2, 6, 3, 7)

ENGS = ("pe", "act", "dve", "pool", "sp")
SEM_EPOCH = 30000


class Buf:
    __slots__ = ("name", "w", "r", "rd", "excl")

    def __init__(self, name, excl=False):
        self.name = name
        self.excl = excl
        self.w = None
        self.r = {}
        self.rd = []


class Op:
    __slots__ = ("eng", "fn", "deps", "marked", "cnt", "dma", "dkey", "dcnt")

    def __init__(self, eng, fn, dma=False, dkey=None):
        self.eng = eng
        self.fn = fn
        self.deps = []
        self.marked = False
        self.cnt = 0
        self.dma = dma
        self.dkey = dkey
        self.dcnt = 0


class Sched:
    def __init__(self):
        self.ops = {e: [] for e in ENGS}
        self.dma_counts = {}
        self.total_keys = set()

    def op(self, eng, fn, reads=(), writes=(), dma=False, dkey=None):
        o = Op(eng, fn, dma, dkey)
        deps = []
        for b in reads:
            if b.w is not None:
                deps.append(b.w)
            if b.excl:
                deps.extend(v for k, v in b.r.items() if k != eng)
        for b in writes:
            if b.w is not None:
                deps.append(b.w)
            deps.extend(b.r.values())
            deps.extend(b.rd)
        for b in writes:
            b.w = o
            b.r = {}
            b.rd = []
        for b in reads:
            if b.w is not o:
                if dma:
                    b.rd.append(o)
                else:
                    b.r[eng] = o
        seen = set()
        for d in deps:
            if d is o or id(d) in seen:
                continue
            seen.add(id(d))
            if d.eng == "pe" and eng == "pe" and not d.dma and not dma:
                continue
            if dma and d.dma and d.dkey == dkey and dkey in self.total_keys:
                continue
            o.deps.append(d)
        if dma:
            self.dma_counts[dkey] = self.dma_counts.get(dkey, 0) + 1
            o.dcnt = self.dma_counts[dkey]
        self.ops[eng].append(o)
        return o

    def emit(self, nc):
        for e in ENGS:
            for o in self.ops[e]:
                for d in o.deps:
                    if not d.dma:
                        d.marked = True
        nsem = {}
        for e in ENGS:
            c = 0
            for o in self.ops[e]:
                if o.marked and not o.dma:
                    c += 1
                    o.cnt = c
            nsem[e] = max(1, (c + SEM_EPOCH - 1) // SEM_EPOCH)
        with contextlib.ExitStack() as st:
            esem = {e: [st.enter_context(nc.semaphore("s_%s_%d" % (e, i))) for i in range(nsem[e])] for e in ENGS}
            dsem = {k: st.enter_context(nc.semaphore("d_%s" % str(k))) for k in self.dma_counts}
            block = st.enter_context(nc.Block())
            engobj = {"pe": "tensor", "act": "scalar", "dve": "vector", "pool": "gpsimd", "sp": "sync"}

            def body(e):
                def f(eng):
                    waited = {}
                    for o in self.ops[e]:
                        for d in o.deps:
                            if d.dma:
                                key = ("d", d.dkey)
                                val = 16 * (self.dma_counts[d.dkey] if d.dkey in self.total_keys else d.dcnt)
                                sem = dsem[d.dkey]
                            else:
                                ep = (d.cnt - 1) // SEM_EPOCH
                                key = ("e", d.eng, ep)
                                val = d.cnt - ep * SEM_EPOCH
                                sem = esem[d.eng][ep]
                            if waited.get(key, 0) >= val:
                                continue
                            waited[key] = val
                            eng.wait_ge(sem, val)
                        ins = o.fn(eng)
                        if o.dma:
                            ins.then_inc(dsem[o.dkey], 16)
                        elif o.marked:
                            ins.then_inc(esem[e][(o.cnt - 1) // SEM_EPOCH], 1)
                    if e == "sp":
                        for k, c in self.dma_counts.items():
                            eng.wait_ge(dsem[k], 16 * c)
                return f

            for e in ENGS:
                getattr(block, engobj[e])(body(e))


class KB:
    def __init__(self, nc, st):
        self.nc = nc
        self.st = st
        self.S = Sched()
        self.ring_i = 0
        self.ps_i = 0
        self.acc_i = 0

    def sb(self, name, shape, dt):
        return self.st.enter_context(self.nc.sbuf_tensor(name, shape, dt))

    def mm(self, out, lhsT, rhs, start, stop, R, W):
        self.S.op("pe", lambda e: e.matmul(out, lhsT=lhsT, rhs=rhs, start=start, stop=stop), R, W)

    def tr(self, out, in_, ident, R, W):
        self.S.op("pe", lambda e: e.transpose(out=out, in_=in_, identity=ident), R, W)

    def act(self, out, in_, func, R, W, scale=None, bias=None):
        kw = {}
        if scale is not None:
            kw["scale"] = scale
        if bias is not None:
            kw["bias"] = bias
        self.S.op("act", lambda e: e.activation(out=out, in_=in_, func=func, **kw), R, W)

    def tt(self, eng, out, in0, in1, op, R, W):
        self.S.op(eng, lambda e: e.tensor_tensor(out=out, in0=in0, in1=in1, op=op), R, W)

    def ts(self, eng, out, in0, s1, s2, op0, op1, R, W):
        if op1 is None:
            self.S.op(eng, lambda e: e.tensor_scalar(out=out, in0=in0, scalar1=s1, scalar2=None, op0=op0), R, W)
        else:
            self.S.op(eng, lambda e: e.tensor_scalar(out=out, in0=in0, scalar1=s1, scalar2=s2, op0=op0, op1=op1), R, W)

    def stt(self, out, in0, scalar, in1, op0, op1, R, W):
        self.S.op("dve", lambda e: e.scalar_tensor_tensor(out=out, in0=in0, scalar=scalar, in1=in1, op0=op0, op1=op1), R, W)

    def cp(self, eng, out, in_, R, W):
        if eng == "act":
            self.S.op("act", lambda e: e.activation(out=out, in_=in_, func=AF.Copy), R, W)
        else:
            self.S.op(eng, lambda e: e.tensor_copy(out=out, in_=in_), R, W)

    def dma(self, out, in_, R, W, key, eng="sp", **kw):
        self.S.op(eng, lambda e: e.dma_start(out=out, in_=in_, **kw), R, W, dma=True, dkey=key)


def build(TP, TS, NPJ=2):
    TO = TS // 2
    nc = bass.Bass("TRN2", target_bir_lowering=False, dynamic_dma_scratch_size=4096)

    def din(name, shape, dt=F32):
        return nc.dram_tensor(name, list(shape), dt, kind="ExternalInput").ap()

    xp = din("xp", [NPJ, TP, D]); xs = din("xs", [TS, D])
    memp = din("memp", [NPJ, 256, D]); mems = din("mems", [256, D])
    rope_p = din("rope_p", [TP, 128]); rope_s = din("rope_s", [TS, 128])
    wd = {}
    wshape = {"w_in": [D, IN_W], "w_pa": [512, D], "w_pb": [512, D], "w_out": [D, D], "w_xq": [D, D],
              "w_xk": [D, D], "w_xv": [D, D], "w_xo": [D, D], "w_up": [D, 4 * D], "w_down": [4 * D, D]}
    wbf = {}
    for n, s in wshape.items():
        wd[n] = din(n, s)
        wbf[n] = nc.dram_tensor(n + "_bf", list(s), BF16, kind="Internal").ap()
    gq = din("gq", [1, 64]); gk = din("gk", [1, 64]); gn = din("gn", [1, 128])
    lbin = din("lbin", [4, 512])
    lnp = din("lnp", [6, D])
    cst = din("cst", [128, 5 * 128 + 4])
    yp = nc.dram_tensor("yp", [NPJ, TP, D], F32, kind="ExternalOutput").ap()
    ys = nc.dram_tensor("ys", [TO, D], F32, kind="ExternalOutput").ap()

    NKT = TS // 128
    NOT_ = max(TO, TP) // 128

    with contextlib.ExitStack() as st:
        kb = KB(nc, st)
        S = kb.S
        sb = kb.sb
        KT = sb("KT", [128, NKT * 128], BF16); bKT = Buf("KT")
        VV = sb("VV", [128, NKT, 192], BF16); bVV = Buf("VV")
        OBW = sb("OBW", [128, NOT_, 512], BF16); bOBW = [Buf("OBW%d" % i) for i in range(NOT_)]
        KmT = sb("KmT", [128, 8, 256], BF16); bKmT = Buf("KmT")
        Vm = sb("Vm", [128, 2, D], BF16); bVm = Buf("Vm")
        LNP = sb("LNP", [128, 6, D], F32); bC = Buf("consts")
        LB = sb("LB", [128, 4, 512], F32)
        GQ = sb("GQ", [128, 64], F32); GK = sb("GK", [128, 64], F32); GN = sb("GN", [128, 128], F32)
        CST = sb("CST", [128, 5 * 128 + 4], F32)
        IDB = sb("IDB", [128, 128], BF16); ONES = sb("ONES", [128, 128], BF16)
        IDF = CST[:, 0:128]
        Wc = {0: CST[:, 128:256], 1: CST[:, 256:384]}
        Mk = {0: CST[:, 384:512], 1: CST[:, 512:640]}
        Wme = {0: CST[:, 640:642], 1: CST[:, 642:644]}
        NR = 3
        RING = [sb("RING%d" % i, [128, 4096], BF16) for i in range(NR)]
        bRING = [Buf("ring%d" % i) for i in range(NR)]
        XRES = sb("XRES", [128, 4, D], F32); bXRES = [Buf("xres%d" % i) for i in range(4)]
        XT = sb("XT", [128, 8, 512], BF16); bXT = Buf("XT")
        RR = sb("RR", [128, 12288], BF16)
        bRR = Buf("RR")
        QT = RR[:, 0:2048].rearrange("p (j t) -> p j t", t=512); bQT = Buf("QT")
        OAT = RR[:, 2048:4096].rearrange("p (j t) -> p j t", t=512); bOAT = Buf("OAT")
        OXT = RR[:, 0:4096].rearrange("p (j t) -> p j t", t=512)
        OBT = RR[:, 4096:6144].rearrange("p (j t) -> p j t", t=512); bOBT = Buf("OBT")
        MG = RR[:, 6144:10240].rearrange("p (j t) -> p j t", t=512); bMG = Buf("MG")
        PT = [RR[:, 10240 + 512 * i:10240 + 512 * (i + 1)] for i in range(4)]; bPT = [Buf("PT%d" % i) for i in range(4)]
        HT = RR[:, 0:8192].rearrange("p (j t) -> p j t", t=512)
        QZ = MG
        SCR = sb("SCR", [128, 8, 512], F32); bSCR = [Buf("scr%d" % i) for i in range(8)]
        LBT = SCR[:, 0:4, :]
        SCB = sb("SCB", [128, 8, 512], BF16); bSCB = [Buf("scb%d" % i) for i in range(8)]
        SST = sb("SST", [128, 2, 512], F32); bSST = [Buf("sst0"), Buf("sst1")]
        ROPE = sb("ROPE", [128, 4, 2, 64], F32); bROPE = [Buf("rope%d" % i) for i in range(4)]
        SM = sb("SM", [128, 96], F32); bSM = Buf("SM")
        bSMb = Buf("SMb"); bSMd = Buf("SMd"); bSMc = Buf("SMc"); bSMe = Buf("SMe")
        HSF = [RING[i][:, :].bitcast(F32)[:, 512 * q:512 * (q + 1)] for i in range(NR) for q in range(4)]
        bHS = [Buf("hs%d" % i) for i in range(12)]
        HSB9 = RING[2][:, 2048:2560]
        DUM = sb("DUM", [128, 2], F32)
        XIN = XRES; bXIN = [bXRES[0], bXRES[1]]
        XTTs = [XRES[:, 2 + i, :].bitcast(BF16)[:, 0:1024].rearrange("p (k t) -> p k t", t=128) for i in range(2)]
        bXTT = [bXRES[2], bXRES[3]]
        XTF = XT[:].rearrange("p k t -> p (k t)")
        HWq = RR[:, 0:4096].rearrange("p (k c) -> p k c", c=512); bHWq = [bQT, bOAT]
        HWf = RR[:, 4096:8192].rearrange("p (k c) -> p k c", c=512); bHWf = [bOBT, bMG]
        HWi = RR[:, 8192:12288].rearrange("p (k c) -> p k c", c=512); bHWi = [bMG] + bPT
        HWk = XTF[:, 0:2048].rearrange("p (k c) -> p k c", c=256)
        HWg = XTF[:, 0:4096].rearrange("p (k c) -> p k c", c=512)
        PSB = [st.enter_context(nc.psum_tensor("PS%d" % i, [128, 512], F32)) for i in range(8)]
        bPS = [Buf("ps%d" % i, excl=True) for i in range(8)]
        bW = Buf("wscratch")

        state = {"ps": 0, "acc": 0, "ring": 0, "xin": 0, "rope": 0, "pt": 0, "main": False}

        def psg():
            if state["main"]:
                i = state["ps"] % 4
            else:
                i = state["ps"] % 8
            state["ps"] += 1
            return PSB[i], bPS[i]

        def psacc():
            i = 4 + state["acc"] % 4
            state["acc"] += 1
            return PSB[i], bPS[i]

        def wload(dram3d, shape):
            i = state["ring"] % NR
            state["ring"] += 1
            a, b = shape
            dst = RING[i][:, 0:a * b].rearrange("p (a b) -> p a b", b=b)
            kb.dma(dst, dram3d, [bW], [bRING[i]], ("ring", i))
            return dst, bRING[i]

        S.total_keys.add("wcast")
        S.total_keys.add("const")
        for n, s in wshape.items():
            tot = s[0] * s[1]
            rows = tot // 512
            src = wd[n].rearrange("a (b f) -> (a b) f", f=512) if s[1] % 512 == 0 else None
            if src is None:
                src = wd[n].rearrange("a (b f) -> (a b) f", f=256)
                dst = wbf[n].rearrange("a (b f) -> (a b) f", f=256)
                rows = tot // 256
            else:
                dst = wbf[n].rearrange("a (b f) -> (a b) f", f=512)
            r0 = 0
            while r0 < rows:
                r1 = min(rows, r0 + 2048)
                kb.dma(dst[r0:r1, :], src[r0:r1, :], [], [bW], "wcast", eng="pool")
                r0 = r1
        kb.dma(CST[:], cst[:, :], [], [bC], "const")
        for i in range(6):
            kb.dma(LNP[:, i, :], lnp[i:i + 1, :].broadcast_to([128, D]), [], [bC], "const")
        for i in range(4):
            kb.dma(LBT[:, i, :], lbin[i:i + 1, :].broadcast_to([128, 512]), [], [bSCR[i]], "const")
        kb.dma(GQ[:], gq[0:1, :].broadcast_to([128, 64]), [], [bC], "const")
        kb.dma(GK[:], gk[0:1, :].broadcast_to([128, 64]), [], [bC], "const")
        kb.dma(GN[:], gn[0:1, :].broadcast_to([128, 128]), [], [bC], "const")
        bC2 = Buf("consts2")
        for d in range(2):
            kb.tt("dve", LBT[:, 2 * d, :], LBT[:, 2 * d, :], LBT[:, 2 * d + 1, :], ALU.subtract, [bSCR[2 * d], bSCR[2 * d + 1]], [bSCR[2 * d]])
            kb.act(LB[:, 2 * d, :], LBT[:, 2 * d, :], AF.Sigmoid, [bSCR[2 * d]], [bC2])
            kb.act(LB[:, 2 * d + 1, :], LBT[:, 2 * d, :], AF.Sigmoid, [bSCR[2 * d]], [bC2], scale=-1.0)
        kb.cp("dve", IDB[:], IDF, [bC], [bC2])
        S.op("pool", lambda e: e.memset(ONES[:], 1.0), [], [bC2])
        S.op("pool", lambda e: e.memset(VV[:, :, 64:128], 1.0), [], [bVV])

        def load_x_transpose(xtile_ap, xbuf, R_extra, dst_fn, dst_buf):
            for g in range(2):
                ps, bps = psg()
                for kk in range(4):
                    k = 4 * g + kk
                    kb.tr(ps[:, kk * 128:(kk + 1) * 128], xtile_ap[:, k * 128:(k + 1) * 128], IDF, [xbuf, bC] + R_extra, [bps])
                kb.cp("act" if g == 0 else "dve", dst_fn(g), ps[:].rearrange("p (k t) -> p k t", t=128), [bps], [dst_buf])

        def proj_tok(ps_ap, bps, xt_fn, xbuf, w3, wbuf, ncols):
            for k in range(8):
                kb.mm(ps_ap[:, 0:ncols], xt_fn(k), w3[:, k, 0:ncols], k == 0, k == 7, [xbuf] + (wbuf if isinstance(wbuf, list) else [wbuf]), [bps])

        def gen_norm_rope(zps, bz, nh, G, rope_i, out_ap, out_buf, scr, so=0, bsm=None):
            bsm = bsm or bSM
            n = nh * 64
            (A0, b0), (A1, b1), (A2, b2) = scr
            T0 = A0[:, 0:n]; T1 = A1[:, 0:n]; T2 = A2[:, 0:n]
            kb.act(T0, zps[:, 0:n], AF.Square, [bz], [b0])
            ss = SM[:, so:so + nh]; rs = SM[:, 16 + so:16 + so + nh]
            yield
            kb.tt("dve", T1.rearrange("p (h d) -> p h d", d=64), zps[:, 0:n].rearrange("p (h d) -> p h d", d=64),
                  G[:, 0:64].unsqueeze(1).broadcast_to([128, nh, 64]), ALU.mult, [bz, bC], [b1])
            S.op("dve", lambda e: e.tensor_reduce(out=ss, in_=T0.rearrange("p (h d) -> p h d", d=64), axis=AX.X, op=ALU.add), [b0], [bsm])
            kb.ts("dve", ss, ss, 1.0 / 64.0, EPS, ALU.mult, ALU.add, [bsm], [bsm])
            kb.act(ss, ss, AF.Ln, [bsm], [bsm])
            kb.act(rs, ss, AF.Exp, [bsm], [bsm], scale=-0.5)
            yield
            Cb = ROPE[:, rope_i, 0, :].unsqueeze(1).broadcast_to([128, nh, 64])
            kb.tt("pool", T0.rearrange("p (h d) -> p h d", d=64), T1.rearrange("p (h d) -> p h d", d=64), Cb, ALU.mult,
                  [b1, bROPE[rope_i]], [b0])
            x5 = T1.rearrange("p (h a b d) -> p h a b d", a=2, b=2, d=16)
            r5 = T2.rearrange("p (h a b d) -> p h a b d", a=2, b=2, d=16)
            sg = ROPE[:, rope_i, 1, :].rearrange("p (a b d) -> p a b d", a=2, b=2)
            for b_ in range(2):
                sgb = sg[:, :, b_, :].unsqueeze(1).broadcast_to([128, nh, 2, 16])
                kb.tt("dve", r5[:, :, :, b_, :], x5[:, :, :, 1 - b_, :], sgb, ALU.mult, [b1, bROPE[rope_i]], [b2])
            yield
            kb.tt("pool", T0, T0, T2, ALU.add, [b0, b2], [b0])
            yield
            kb.tt("pool", out_ap.rearrange("p (h d) -> p h d", d=64), T0.rearrange("p (h d) -> p h d", d=64),
                  rs.unsqueeze(2).broadcast_to([128, nh, 64]), ALU.mult, [b0, bsm], [out_buf])

        def norm_rope(*a, **k):
            for _ in gen_norm_rope(*a, **k):
                pass

        def layer_norm(tile, ln_i):
            xr = XRES[:, tile, :]
            st6 = SM[:, 32:44]; mv = SM[:, 44:46]; rstd = SM[:, 46:47]
            for h in range(2):
                S.op("dve", lambda e, h=h: e.bn_stats(out=SM[:, 32 + 6 * h:38 + 6 * h], in_=xr[:, h * 512:(h + 1) * 512]), [bXRES[tile]], [bSMc])
            S.op("dve", lambda e: e.bn_aggr(out=mv, in_=st6), [bSMc], [bSMc])
            kb.ts("dve", rstd, SM[:, 45:46], EPS, None, ALU.add, None, [bSMc], [bSMc])
            kb.act(rstd, rstd, AF.Ln, [bSMc], [bSMc])
            kb.act(rstd, rstd, AF.Exp, [bSMc], [bSMc], scale=-0.5)
            kb.stt(xr, xr, SM[:, 44:45], LNP[:, 2 * ln_i, :], ALU.subtract, ALU.mult, [bXRES[tile], bSMc, bC], [bXRES[tile]])
            kb.stt(xr, xr, rstd, LNP[:, 2 * ln_i + 1, :], ALU.mult, ALU.add, [bXRES[tile], bSMc, bC], [bXRES[tile]])

        def roundrobin(*gens):
            gl = [g for g in gens if g is not None]
            while gl:
                for g in list(gl):
                    try:
                        next(g)
                    except StopIteration:
                        gl.remove(g)

        def F(i):
            return (SCR[:, i, :], bSCR[i]) if i < 8 else (HSF[i - 8], bHS[i - 8])
        T_F = F(0)
        T_LF = [F(1), F(8)]
        T_K = [F(2), F(9)]
        T_Q = [F(3), F(10)]
        T_EP = F(4)
        T_EM = F(5)
        T_R0 = F(6)
        T_R1 = F(7)
        NRS = [F(11), F(12), F(13)]
        T_TU = F(14)
        T_TS = F(17)
        T_G = [F(15), F(16)]
        B_V = [(SCB[:, 0, :], bSCB[0]), (HSB9, bHS[9])]
        B_QE, B_KE, B_QET, B_KET, B_ATM, B_SP = [(SCB[:, i, :], bSCB[i]) for i in range(1, 7)]
        CE = SM[:, 48:56]
        CF = SM[:, 56:60]

        def gen_A(sweep, tile, own, par, x_ap, rope_ap, W, nxt):
            wkv, bwkv, wqh, bwqh, wzf, bwzf, wih, bwih, wgh, bwgh = W
            xi = par
            load_x_transpose(XIN[:, xi, :], bXIN[xi], [], lambda g: XTTs[xi][:, 4 * g:4 * g + 4, :], bXTT[xi])
            xt_fn = lambda k: XTTs[xi][:, k, :]
            if nxt is not None:
                kb.dma(XIN[:, 1 - xi, :], x_ap[nxt * 128:(nxt + 1) * 128, :], [], [bXIN[1 - xi]], ("xin", 1 - xi))
            yield
            if sweep == 1:
                pkv, bpkv = psg()
                proj_tok(pkv, bpkv, xt_fn, bXTT[xi], wkv, bwkv, 256)
                ri = state["rope"] % 4; state["rope"] += 1
                kb.dma(ROPE[:, ri, :, :].rearrange("p c d -> p (c d)"), rope_ap[tile * 128:(tile + 1) * 128, :],
                       [], [bROPE[ri]], ("rope", ri))
                yield
                KB_ = SCB[:, 7, 0:128]
                norm_rope(pkv, bpkv, 2, GK, ri, KB_, bSCB[7], NRS)
                kb.cp("act", VV[:, tile, 0:64], pkv[:, 128:192], [bpkv], [bVV])
                kb.cp("act", VV[:, tile, 128:192], pkv[:, 192:256], [bpkv], [bVV])
                yield
            if own:
                zq, bzq = psg()
                proj_tok(zq, bzq, xt_fn, bXTT[xi], wqh, bwqh, 512)
                yield
            zf, bzf = psg()
            proj_tok(zf, bzf, xt_fn, bXTT[xi], wzf, bwzf, 512)
            if sweep == 0:
                zg, bzg = psg()
                proj_tok(zg, bzg, xt_fn, bXTT[xi], wgh, bwgh, 512)
            if own:
                kb.act(T_Q[par][0], zq[:], AF.Silu, [bzq], [T_Q[par][1]])
            if sweep == 0:
                kb.act(T_G[par][0], zg[:], AF.Silu, [bzg], [T_G[par][1]])
            kb.act(T_F[0], zf[:], AF.Sigmoid, [bzf], [T_F[1]])
            yield
            if sweep == 1:
                pt_, bpt_ = psg()
                ptb = pt_[:].bitcast(BF16)
                kb.tr(ptb[:, 0:128], SCB[:, 7, 0:128], IDB[:], [bSCB[7], bC2], [bpt_])
                kb.cp("dve", KT[:, tile * 128:(tile + 1) * 128], ptb[:, 0:128], [bpt_], [bKT])
            zi, bzi = psg()
            proj_tok(zi, bzi, xt_fn, bXTT[xi], wih, bwih, 512)
            kb.tt("dve", T_F[0], T_F[0], LB[:, 2 * sweep + 1, :], ALU.mult, [T_F[1], bC2], [T_F[1]])
            kb.tt("pool", T_F[0], T_F[0], LB[:, 2 * sweep, :], ALU.add, [T_F[1], bC2], [T_F[1]])
            kb.act(T_LF[par][0], T_F[0], AF.Ln, [T_F[1]], [T_LF[par][1]])
            kb.ts("pool", T_K[par][0], T_F[0], -1.0, 1.0, ALU.mult, ALU.add, [T_F[1]], [T_K[par][1]])
            kb.cp("act", B_V[par][0], zi[:], [bzi], [B_V[par][1]])
            if sweep == 0:
                tg = T_G[par]
                kb.tt("dve", tg[0].rearrange("p (h d) -> p h d", d=128), tg[0].rearrange("p (h d) -> p h d", d=128),
                      GN[:].unsqueeze(1).broadcast_to([128, 4, 128]), ALU.mult, [tg[1], bC], [tg[1]])
            yield

        def gen_B(dr, tile, own, par, first):
            lf, blf = T_LF[par]
            tk, btk = T_K[par]
            tq, btq = T_Q[par]
            bv, bbv = B_V[par]
            pb, bpb = psg()
            kb.mm(pb[:], Wc[dr], lf, True, True, [bC, blf], [bpb])
            pc, bpc = psg()
            for h in range(4):
                kb.mm(pc[:, 2 * h:2 * h + 2], lf[:, h * 128:(h + 1) * 128], Wme[dr], True, True, [bC, blf], [bpc])
            yield
            kb.act(CE, pc[:, 0:8], AF.Exp, [bpc], [bSMb])
            ce3 = CE.rearrange("p (h c) -> p h c", c=2)
            kb.tt("dve", CF, ce3[:, :, 0], ce3[:, :, 1], ALU.mult, [bSMb], [bSMb])
            kb.act(T_EM[0], pb[:], AF.Exp, [bpb], [T_EM[1]], scale=-1.0)
            kb.tt("pool", B_KE[0], tk, T_EM[0], ALU.mult, [btk, T_EM[1]], [B_KE[1]])
            Ops = bO = None
            if own:
                kb.act(T_EP[0], pb[:], AF.Exp, [bpb], [T_EP[1]])
                kb.tt("pool", B_QE[0], tq, T_EP[0], ALU.mult, [btq, T_EP[1]], [B_QE[1]])
                yield
                for (src, dst, eng_) in ((B_QE, B_QET, "act"), (B_KE, B_KET, "dve")):
                    ps, bps = psg()
                    psb = ps[:].bitcast(BF16)
                    for h in range(4):
                        kb.tr(psb[:, h * 128:(h + 1) * 128], src[0][:, h * 128:(h + 1) * 128], IDB[:], [src[1], bC2], [bps])
                    kb.cp(eng_, dst[0], psb[:, 0:512], [bps], [dst[1]])
                yield
                pa, bpa = psg()
                for h in range(4):
                    hs = slice(h * 128, (h + 1) * 128)
                    kb.mm(pa[:, hs], B_KET[0][:, hs], B_QET[0][:, hs], True, True, [B_KET[1], B_QET[1]], [bpa])
                kb.tt("dve", B_ATM[0].rearrange("p (h t) -> p h t", t=128), pa[:].rearrange("p (h t) -> p h t", t=128),
                      Mk[dr].unsqueeze(1).broadcast_to([128, 4, 128]), ALU.mult, [bpa, bC], [B_ATM[1]])
                if not first:
                    kb.tt("pool", B_SP[0].rearrange("p (h e) -> p h e", e=128), SST[:, dr, :].rearrange("p (h e) -> p h e", e=128),
                          ce3[:, :, 0:1].broadcast_to([128, 4, 128]), ALU.mult, [bSST[dr], bSMb], [B_SP[1]])
                yield
                Ops, bO = psg()
                for h in range(4):
                    hs = slice(h * 128, (h + 1) * 128)
                    kb.mm(Ops[:, hs], B_ATM[0][:, hs], bv[:, hs], True, first, [B_ATM[1], bbv], [bO])
                    if not first:
                        kb.mm(Ops[:, hs], B_QET[0][:, hs], B_SP[0][:, hs], False, True, [B_QET[1], B_SP[1]], [bO])
            yield
            pk, bpk = psg()
            for h in range(4):
                hs = slice(h * 128, (h + 1) * 128)
                kb.mm(pk[:, hs], B_KE[0][:, hs], bv[:, hs], True, True, [B_KE[1], bbv], [bpk])
            cend_b = ce3[:, :, 1:2].broadcast_to([128, 4, 128])
            pk3 = pk[:].rearrange("p (h e) -> p h e", e=128)
            S3 = SST[:, dr, :].rearrange("p (h e) -> p h e", e=128)
            if first:
                kb.tt("dve", S3, pk3, cend_b, ALU.mult, [bpk, bSMb], [bSST[dr]])
            else:
                kb.tt("dve", T_TU[0].rearrange("p (h e) -> p h e", e=128), pk3, cend_b, ALU.mult, [bpk, bSMb], [T_TU[1]])
                kb.tt("pool", T_TS[0].rearrange("p (h e) -> p h e", e=128), S3, CF.unsqueeze(2).broadcast_to([128, 4, 128]), ALU.mult,
                      [bSST[dr], bSMb], [T_TS[1]])
                kb.tt("pool", SST[:, dr, :], T_TS[0], T_TU[0], ALU.add, [T_TS[1], T_TU[1]], [bSST[dr]])
            yield
            if own and dr == 1:
                kb.cp("act", OBW[:, tile, :], Ops[:], [bO], [bOBW[tile]])
            if dr == 0:
                T0, b0 = T_R0
                T1, b1 = T_R1
                kb.tt("dve", T0, Ops[:], OBW[:, tile, :], ALU.add, [bO, bOBW[tile]], [b0])
                kb.tt("pool", T1, T0, T0, ALU.mult, [b0], [b1])
                ss = SM[:, 64:68]; rs = SM[:, 72:76]
                S.op("dve", lambda e: e.tensor_reduce(out=ss, in_=T1.rearrange("p (h d) -> p h d", d=128), axis=AX.X, op=ALU.add),
                     [b1], [bSMd])
                kb.ts("dve", ss, ss, 1.0 / 128.0, EPS, ALU.mult, ALU.add, [bSMd], [bSMd])
                kb.act(ss, ss, AF.Ln, [bSMd], [bSMd])
                kb.act(rs, ss, AF.Exp, [bSMd], [bSMd], scale=-0.5)
                kb.tt("pool", T0.rearrange("p (h d) -> p h d", d=128), T0.rearrange("p (h d) -> p h d", d=128),
                      rs.unsqueeze(2).broadcast_to([128, 4, 128]), ALU.mult, [b0, bSMd], [b0])
                kb.tt("pool", OBW[:, tile, :], T0, T_G[par][0], ALU.mult, [b0, T_G[par][1]], [bOBW[tile]])

        def run_job(ji, x_ap, T_all, T_own, mem_ap, rope_ap, out_ap):
            n_all = T_all // 128
            n_own = T_own // 128
            nblk_all = T_all // 512
            nblk_own = T_own // 512
            state["main"] = False
            for mt in range(2):
                xi = state["xin"] % 2; state["xin"] += 1
                kb.dma(XIN[:, xi, :], mem_ap[mt * 128:(mt + 1) * 128, :], [], [bXIN[xi]], ("xin", xi))
                load_x_transpose(XIN[:, xi, :], bXIN[xi], [],
                                 lambda g, mt=mt: XT[:, 4 * g:4 * g + 4, mt * 128:(mt + 1) * 128], bXT)
            for half in range(2):
                w3, wb_ = wload(wbf["w_xk"][:, half * 512:(half + 1) * 512].rearrange("(k p) c -> p k c", p=128), (8, 512))
                for mm_ in range(4):
                    m = half * 4 + mm_
                    ps, bps = psg()
                    for k in range(8):
                        kb.mm(ps[:, 0:256], w3[:, k, mm_ * 128:(mm_ + 1) * 128], XT[:, k, 0:256], k == 0, k == 7, [wb_, bXT], [bps])
                    kb.cp("act", KmT[:, m, :], ps[:, 0:256], [bps], [bKmT])
            for half in range(2):
                w3, wb_ = wload(wbf["w_xv"][:, half * 512:(half + 1) * 512].rearrange("(k p) c -> p k c", p=128), (8, 512))
                for mt in range(2):
                    ps, bps = psg()
                    for k in range(8):
                        kb.mm(ps[:], XT[:, k, mt * 128:(mt + 1) * 128], w3[:, k, :], k == 0, k == 7, [wb_, bXT], [bps])
                    kb.cp("dve", Vm[:, mt, half * 512:(half + 1) * 512], ps[:], [bps], [bVm])

            S.op("dve", lambda e: e.memset(DUM[:, 0:1], 0.0), [], bRING + bHS)
            for sweep in (1, 0):
                blocks = list(range(nblk_all)) if sweep == 1 else list(range(nblk_own))
                if sweep == 1:
                    blocks = blocks[::-1]
                first = True
                win = wbf["w_in"]
                wv = lambda c0, n: win[:, c0:c0 + n].rearrange("(k p) c -> p k c", p=128)
                if sweep == 1:
                    kb.dma(HWq, wv(C_QH, 512), [bW], bHWq, ("hw", 0))
                    kb.dma(HWf, wv(C_ZB, 512), [bW], bHWf, ("hw", 1))
                    kb.dma(HWi, wv(C_IH, 512), [bW], bHWi, ("hw", 2))
                    kb.dma(HWk, wv(C_KA, 256), [bW], [bXT], ("hw", 3))
                else:
                    kb.dma(HWf, wv(C_ZF, 512), [bW], bHWf, ("hw", 1))
                    kb.dma(HWg, wv(C_GH, 512), [bW], [bXT], ("hw", 3))
                wkv, bwkv = HWk, [bXT]
                wqh, bwqh = HWq, bHWq
                wzf, bwzf = HWf, bHWf
                wih, bwih = HWi, bHWi
                wgh, bwgh = HWg, [bXT]
                order = []
                for blk in blocks:
                    tl = list(range(4 * blk, 4 * blk + 4))
                    order.extend(tl[::-1] if sweep == 1 else tl)
                prevB = None
                kb.dma(XIN[:, 0, :], x_ap[order[0] * 128:(order[0] + 1) * 128, :], [], [bXIN[0]], ("xin", 0))
                for idx, tile in enumerate(order):
                    par = idx % 2
                    nxt = order[idx + 1] if idx + 1 < len(order) else None
                    gA = gen_A(sweep, tile, tile < n_own, par, x_ap, rope_ap, (wkv, bwkv, wqh, bwqh, wzf, bwzf, wih, bwih, wgh, bwgh), nxt)
                    roundrobin(gA, prevB)
                    prevB = gen_B(sweep, tile, tile < n_own, par, idx == 0)
                roundrobin(prevB)
            S.op("dve", lambda e: e.memset(DUM[:, 1:2], 0.0), [], bRING + bHS)

            state["main"] = True
            for blk in range(nblk_own):
                t0 = blk * 512
                if blk == 0:
                    for t in range(4):
                        kb.dma(XRES[:, t, :], x_ap[t0 + t * 128:t0 + (t + 1) * 128, :], [], [bXRES[t]], ("xres", t))
                for t in range(4):
                    load_x_transpose(XRES[:, t, :], bXRES[t], [], lambda g, t=t: XT[:, 4 * g:4 * g + 4, t * 128:(t + 1) * 128], bXT)
                qz4_ = QZ.rearrange("p (j h) t -> p j h t", h=2)
                S.op("pool", lambda e, a=qz4_[64:128, :, 0, :]: e.memset(a, 0.0), [], [bMG])
                S.op("pool", lambda e, a=qz4_[0:64, :, 1, :]: e.memset(a, 0.0), [], [bMG])
                wq, bwq = wload(wbf["w_in"][:, C_QA:C_QA + 512].rearrange("(k p) c -> p k c", p=128), (8, 512))
                pqs = []
                for t in range(4):
                    pq, bpq = psg()
                    proj_tok(pq, bpq, lambda k, t=t: XT[:, k, t * 128:(t + 1) * 128], bXT, wq, bwq, 512)
                    ri = state["rope"] % 4; state["rope"] += 1
                    kb.dma(ROPE[:, ri, :, :].rearrange("p c d -> p (c d)"), rope_ap[t0 + t * 128:t0 + (t + 1) * 128, :],
                           [], [bROPE[ri]], ("rope", ri))
                    pqs.append((pq, bpq, ri))
                for t in range(4):
                    tile = blk * 4 + t
                    ps, bps = psacc()
                    psb = ps[:].bitcast(BF16)
                    for h in range(4):
                        kb.tr(psb[:, h * 128:(h + 1) * 128], OBW[:, tile, h * 128:(h + 1) * 128], IDB[:], [bOBW[tile], bC2], [bps])
                    kb.cp("act", OBT[:, :, t * 128:(t + 1) * 128], psb[:, 0:512].rearrange("p (j t) -> p j t", t=128), [bps], [bOBT])
                nrscr = [[(SCR[:, 6, :], bSCR[6]), (SCR[:, 7, :], bSCR[7]), (SCR[:, 5, :], bSCR[5])],
                         [(SCR[:, 2, :], bSCR[2]), (SCR[:, 3, :], bSCR[3]), (SCR[:, 0, :], bSCR[0])]]
                for pr in range(2):
                    gl = []
                    for q_ in range(2):
                        t = 2 * pr + q_
                        pq, bpq, ri = pqs[t]
                        gl.append(gen_norm_rope(pq, bpq, 8, GQ, ri, SCB[:, 7 - q_, :], bSCB[7 - q_], nrscr[q_], so=8 * q_,
                                                bsm=(bSM if q_ == 0 else bSMe)))
                    roundrobin(*gl)
                    for q_ in range(2):
                        t = 2 * pr + q_
                        ps, bps = psacc()
                        psb = ps[:].bitcast(BF16)
                        for j in range(4):
                            kb.tr(psb[:, j * 128:(j + 1) * 128], SCB[:, 7 - q_, j * 128:(j + 1) * 128], IDB[:], [bSCB[7 - q_], bC2], [bps])
                        tc_ = slice(t * 128, (t + 1) * 128)
                        ps3 = psb[:, 0:512].rearrange("p (j t) -> p j t", t=128)
                        qz4 = QZ.rearrange("p (j h) t -> p j h t", h=2)
                        kb.cp("act", qz4[0:64, :, 0, tc_], ps3[0:64, :, :], [bps], [bMG])
                        kb.cp("act", qz4[64:128, :, 1, tc_], ps3[64:128, :, :], [bps], [bMG])
                RCP = SCR[:, 4, :]
                steps = [(j, half, kt) for j in range(4) for half in range(2) for kt in range(n_all)]
                LOOK = 2
                pend = []
                accs = {}
                for si in range(len(steps) + LOOK):
                    if si < len(steps):
                        j, half, kt = steps[si]
                        if kt == 0:
                            accs[(j, half)] = psacc()
                        sp_, bsp = psg()
                        kb.mm(sp_[:], KT[:, kt * 128:(kt + 1) * 128], QZ[:, 2 * j + half, :], True, True, [bKT, bMG], [bsp])
                        pi = state["pt"] % 4; state["pt"] += 1
                        kb.act(PT[pi], sp_[:], AF.Exp, [bsp], [bPT[pi]], scale=0.125)
                        pend.append(pi)
                    if si >= LOOK:
                        j, half, k2 = steps[si - LOOK]
                        pi2 = pend[si - LOOK]
                        oa, boa = accs[(j, half)]
                        kb.mm(oa[:], VV[:, k2, half * 64:half * 64 + 128], PT[pi2], k2 == 0, k2 == n_all - 1, [bVV, bPT[pi2]], [boa])
                        if k2 == n_all - 1:
                            prt = slice(half * 64, half * 64 + 64)
                            oprt = slice(64 - half * 64, 128 - half * 64)
                            S.op("dve", lambda e, oa=oa, oprt=oprt: e.reciprocal(out=RCP[oprt, :], in_=oa[oprt, :]), [boa], [bSCR[4]])
                            kb.tt("dve", OAT[prt, j, :], oa[prt, :], RCP[oprt, :], ALU.mult, [boa, bSCR[4]], [bOAT])
                for br in range(2):
                    wp_, bwp_ = wload(wbf["w_pa" if br == 0 else "w_pb"].rearrange("(k p) c -> p k c", p=128), (4, 1024))
                    src3, bsrc = (OAT, bOAT) if br == 0 else (OBT, bOBT)
                    cg = C_GA if br == 0 else C_GB
                    for mh in range(2):
                        wg_, bwg_ = wload(wbf["w_in"][:, cg + mh * 512:cg + (mh + 1) * 512].rearrange("(k p) c -> p k c", p=128), (8, 512))
                        for mm_ in range(4):
                            m = mh * 4 + mm_
                            pg_, bpg_ = psg()
                            for k in range(8):
                                kb.mm(pg_[:], wg_[:, k, mm_ * 128:(mm_ + 1) * 128], XT[:, k, :], k == 0, k == 7, [bwg_, bXT], [bpg_])
                            pp_, bpp_ = psg()
                            for k in range(4):
                                kb.mm(pp_[:], wp_[:, k, m * 128:(m + 1) * 128], src3[:, k, :], k == 0, k == 3, [bwp_, bsrc], [bpp_])
                            kb.act(SCR[:, 0, :], pg_[:], AF.Sigmoid, [bpg_], [bSCR[0]])
                            if br == 0:
                                kb.tt("dve", MG[:, m, :], pp_[:], SCR[:, 0, :], ALU.mult, [bpp_, bSCR[0]], [bMG])
                            else:
                                kb.tt("dve", SCR[:, 1, :], pp_[:], SCR[:, 0, :], ALU.mult, [bpp_, bSCR[0]], [bSCR[1]])
                                kb.tt("dve", MG[:, m, :], MG[:, m, :], SCR[:, 1, :], ALU.add, [bMG, bSCR[1]], [bMG])

                def xT_of(t):
                    load_x_transpose(XRES[:, t, :], bXRES[t], [], lambda g, t=t: XT[:, 4 * g:4 * g + 4, t * 128:(t + 1) * 128], bXT)

                def out_proj_ln(src3, bsrc, wname, ln_i, nk, after):
                    ws = [wload(wbf[wname][:, half * 512:(half + 1) * 512].rearrange("(k p) c -> p k c", p=128), (nk, 512)) for half in range(2)]
                    for t in range(4):
                        for half in range(2):
                            w3, wb_ = ws[half]
                            ps, bps = psg()
                            for k in range(nk):
                                kb.mm(ps[:], src3[:, k, t * 128:(t + 1) * 128], w3[:, k, :], k == 0, k == nk - 1, bsrc + [wb_], [bps])
                            xr = XRES[:, t, half * 512:(half + 1) * 512]
                            kb.stt(xr, xr, ALPHA, ps[:], ALU.mult, ALU.add, [bXRES[t], bps], [bXRES[t]])
                        layer_norm(t, ln_i)
                        if t >= 1:
                            after(t - 1)
                    after(3)

                out_proj_ln(MG, [bMG], "w_out", 0, 8, xT_of)
                for half in range(2):
                    w3, wb_ = wload(wbf["w_xq"][:, half * 512:(half + 1) * 512].rearrange("(k p) c -> p k c", p=128), (8, 512))
                    for mm_ in range(4):
                        m = half * 4 + mm_
                        ps, bps = psg()
                        for k in range(8):
                            kb.mm(ps[:], w3[:, k, mm_ * 128:(mm_ + 1) * 128], XT[:, k, :], k == 0, k == 7, [wb_, bXT], [bps])
                        kb.cp("act", MG[:, m, :], ps[:], [bps], [bMG])
                for hx in range(4):
                    pis = []
                    for mt in range(2):
                        sp_, bsp = psg()
                        for dc in range(2):
                            kb.mm(sp_[:], KmT[:, 2 * hx + dc, mt * 128:(mt + 1) * 128], MG[:, 2 * hx + dc, :], dc == 0, dc == 1, [bKmT, bMG], [bsp])
                        pi = state["pt"] % 4; state["pt"] += 1
                        kb.act(PT[pi], sp_[:], AF.Exp, [bsp], [bPT[pi]], scale=1.0 / 16.0)
                        pis.append(pi)
                    sm_, bsm_ = psacc()
                    for mt in range(2):
                        kb.mm(sm_[:], ONES[:], PT[pis[mt]], mt == 0, mt == 1, [bC2, bPT[pis[mt]]], [bsm_])
                    S.op("dve", lambda e, sm_=sm_: e.reciprocal(out=SCR[:, 6, :], in_=sm_[:]), [bsm_], [bSCR[6]])
                    for dc in range(2):
                        ox, box = psacc()
                        for mt in range(2):
                            kb.mm(ox[:], Vm[:, mt, (2 * hx + dc) * 128:(2 * hx + dc + 1) * 128], PT[pis[mt]], mt == 0, mt == 1, [bVm, bPT[pis[mt]]], [box])
                        kb.tt("dve", OXT[:, 2 * hx + dc, :], ox[:], SCR[:, 6, :], ALU.mult, [box, bSCR[6]], [bQT, bOAT])
                out_proj_ln(OXT, [bQT, bOAT], "w_xo", 1, 8, xT_of)
                bHT = [bQT, bOAT, bOBT, bMG]
                for ffh in range(2):
                    for c4 in range(4):
                        c0 = ffh * 2048 + c4 * 512
                        w3, wb_ = wload(wbf["w_up"][:, c0:c0 + 512].rearrange("(k p) c -> p k c", p=128), (8, 512))
                        for mm_ in range(4):
                            fc = c4 * 4 + mm_
                            ps, bps = psg()
                            for k in range(8):
                                kb.mm(ps[:], w3[:, k, mm_ * 128:(mm_ + 1) * 128], XT[:, k, :], k == 0, k == 7, [wb_, bXT], [bps])
                            kb.act(SCR[:, 7, :], ps[:], AF.Square, [bps], [bSCR[7]])
                            kb.stt(HT[:, fc, :], ps[:], 0.0, SCR[:, 7, :], ALU.is_gt, ALU.mult, [bps, bSCR[7]], bHT)
                    for half in range(2):
                        accs = [psacc() for _ in range(4)]
                        for c2 in range(2):
                            r0 = ffh * 2048 + c2 * 1024
                            w3, wb_ = wload(wbf["w_down"][r0:r0 + 1024, half * 512:(half + 1) * 512].rearrange("(k p) c -> p k c", p=128), (8, 512))
                            for kk in range(8):
                                k = c2 * 8 + kk
                                for t in range(4):
                                    kb.mm(accs[t][0][:], HT[:, k, t * 128:(t + 1) * 128], w3[:, kk, :], k == 0, k == 15, bHT + [wb_], [accs[t][1]])
                        for t in range(4):
                            xr = XRES[:, t, half * 512:(half + 1) * 512]
                            if ffh == 0:
                                kb.stt(xr, xr, ALPHA, accs[t][0][:], ALU.mult, ALU.add, [bXRES[t], accs[t][1]], [bXRES[t]])
                            else:
                                kb.tt("dve", xr, xr, accs[t][0][:], ALU.add, [bXRES[t], accs[t][1]], [bXRES[t]])
                for t in range(4):
                    layer_norm(t, 2)
                    kb.dma(out_ap[t0 + t * 128:t0 + (t + 1) * 128, :], XRES[:, t, :], [bXRES[t]], [], ("xout", t))
                    if blk + 1 < nblk_own:
                        t1_ = t0 + 512
                        kb.dma(XRES[:, t, :], x_ap[t1_ + t * 128:t1_ + (t + 1) * 128, :], [], [bXRES[t]], ("xres", t))

        for j in range(NPJ):
            run_job(j, xp[j], TP, TP, memp[j], rope_p, yp[j])
        run_job(NPJ, xs, TS, TO, mems, rope_s, ys)
        S.emit(nc)
    return nc


def _rope_tables(pos, T):
    rows = (pos // 64).astype(np.float32)
    cols = (pos % 64).astype(np.float32)
    inv = (np.float32(10000.0) ** (-(np.arange(0, 32, 2, dtype=np.float32)) / np.float32(32))).astype(np.float32)
    ar = (rows[:, None] * inv[None, :]).astype(np.float32)
    ac = (cols[:, None] * inv[None, :]).astype(np.float32)
    cr, sr, cc, sc = np.cos(ar), np.sin(ar), np.cos(ac), np.sin(ac)
    C = np.concatenate([cr, cr, cc, cc], axis=1)
    Sg = np.concatenate([-sr, sr, -sc, sc], axis=1)
    return np.ascontiguousarray(np.concatenate([C, Sg], axis=1).astype(np.float32))


def _consts():
    s = np.arange(128)[:, None]
    t = np.arange(128)[None, :]
    ident = (s == t).astype(np.float32)
    Wf = (s <= t).astype(np.float32) - (s <= 63).astype(np.float32)
    Wb = (s >= t).astype(np.float32) - (s >= 64).astype(np.float32)
    Mf = (s <= t).astype(np.float32)
    Mb = (s >= t).astype(np.float32)
    sv = np.arange(128)
    wme = np.stack([(sv <= 63), (sv >= 64), (sv >= 64), (sv <= 63)], axis=1).astype(np.float32)
    return np.ascontiguousarray(np.concatenate([ident, Wf, Wb, Mf, Mb, wme], axis=1))


_CACHE = {}
_HOOK = {}


def kernel(x_prompt, x_sample, mem_prompt, mem_sample, w_in, w_pa, w_pb, w_out, q_norm, k_norm, hg_lb, hg_gnorm,
           ln1_g, ln1_b, w_xq, w_xk, w_xv, w_xo, ln2_g, ln2_b, w_up, w_down, ln3_g, ln3_b):
    f = lambda a: np.ascontiguousarray(np.asarray(a, dtype=np.float32))
    x_prompt, x_sample, mem_prompt, mem_sample = f(x_prompt), f(x_sample), f(mem_prompt), f(mem_sample)
    NB, TP, _ = x_prompt.shape
    NS, TS, _ = x_sample.shape
    NPJ = NB // NCORES
    assert NS * 2 == NCORES
    key = (TP, TS, NPJ)
    if key not in _CACHE:
        _CACHE[key] = build(TP, TS, NPJ)
    nc = _CACHE[key]
    w_in0 = f(w_in)[0]
    qcols = np.concatenate([np.arange(h * 64, (h + 1) * 64) for h in HEAD_PERM])
    parts = [w_in0[:, 0:512][:, qcols], w_in0[:, 512:768]]
    segs = {"qh": w_in0[:, 768:1280], "zf": w_in0[:, 1280:1792], "zb": w_in0[:, 1792:2304], "rest": w_in0[:, 2304:]}
    w_in_nat = np.ascontiguousarray(np.concatenate(parts + [segs["qh"], segs["zf"], segs["zb"], segs["rest"]], axis=1))
    w_in_rev = np.ascontiguousarray(np.concatenate(parts + [segs["qh"], segs["zb"], segs["zf"], segs["rest"]], axis=1))
    w_pa_p = np.ascontiguousarray(f(w_pa)[0][qcols, :])
    lb = f(hg_lb)
    lb_nat = np.ascontiguousarray(lb.reshape(4, 512))
    lb_rev = np.ascontiguousarray(lb[::-1].reshape(4, 512))
    lnp = np.ascontiguousarray(np.concatenate([f(ln1_g), f(ln1_b), f(ln2_g), f(ln2_b), f(ln3_g), f(ln3_b)], axis=0))
    common = {
        "w_pa": w_pa_p, "w_pb": f(w_pb)[0], "w_out": f(w_out)[0], "w_xq": f(w_xq)[0], "w_xk": f(w_xk)[0],
        "w_xv": f(w_xv)[0], "w_xo": f(w_xo)[0], "w_up": f(w_up)[0], "w_down": f(w_down)[0],
        "gq": np.ascontiguousarray(f(q_norm)[0][None, :]), "gk": np.ascontiguousarray(f(k_norm)[0][None, :]),
        "gn": np.ascontiguousarray(f(hg_gnorm)[0][None, :]), "lnp": lnp, "cst": _consts(),
    }
    TO = TS // 2
    in_maps = []
    for c in range(NCORES):
        rev = (c % 2 == 1)
        s = c // 2
        xp = x_prompt[c * NPJ:(c + 1) * NPJ]
        xs = x_sample[s]
        pp = np.arange(TP)
        ps_ = np.arange(TS)
        if rev:
            xp = xp[:, ::-1, :]
            xs = xs[::-1, :]
            pp = pp[::-1]
            ps_ = ps_[::-1]
        m = dict(common)
        m.update({
            "xp": np.ascontiguousarray(xp), "xs": np.ascontiguousarray(xs),
            "memp": np.ascontiguousarray(mem_prompt[c * NPJ:(c + 1) * NPJ]), "mems": np.ascontiguousarray(mem_sample[s]),
            "rope_p": _rope_tables(pp, TP), "rope_s": _rope_tables(ps_, TS),
            "w_in": w_in_rev if rev else w_in_nat, "lbin": lb_rev if rev else lb_nat,
        })
        in_maps.append(m)
    if _HOOK.get("sim") is not None:
        return _HOOK["sim"](nc, in_maps)
    res = run_bass_kernel_spmd(nc, in_maps, core_ids=list(range(NCORES)))
    y_prompt = np.empty((NB, TP, D), np.float32)
    y_sample = np.empty((NS, TS, D), np.float32)
    for c in range(NCORES):
        r = res.results[c]
        rev = (c % 2 == 1)
        yp = r["yp"]
        ys = r["ys"]
        s = c // 2
        if rev:
            y_prompt[c * NPJ:(c + 1) * NPJ] = yp[:, ::-1, :]
            y_sample[s, TO:] = ys[::-1, :]
        else:
            y_prompt[c * NPJ:(c + 1) * NPJ] = yp
            y_sample[s, :TO] = ys
    return (y_prompt, y_sample)
```
